# Optimizing a Trainium2 kernel written in Bass

```python
import math
import jax, jax.numpy as jnp
from jax import lax
import numpy as np

D_MODEL = 2048
BATCH = 16
SEQ = 2048
DEPTH = 2

HEAD_DIM = 64
N_MIXERS = 4
GROUP_WIDTH = D_MODEL // N_MIXERS
MIX_WIDTH = N_MIXERS * GROUP_WIDTH
SGU_GROUPS = GROUP_WIDTH // HEAD_DIM
SGU_CHUNK = 128
DIL_HEADS = GROUP_WIDTH // HEAD_DIM
DIL_PATTERNS = ((128, 1), (512, 4), (2048, 16))
CONV_CH = GROUP_WIDTH
CONV_WIDTH = 31
GQA_Q_HEADS = GROUP_WIDTH // HEAD_DIM
GQA_KV_HEADS = GQA_Q_HEADS // 4
KV_WIDTH = GQA_KV_HEADS * HEAD_DIM
Q_BLOCK = 128
GRID_W = 64
ROPE_THETA = 10000.0
REL_BUCKETS = 32
REL_MAX_DIST = 1024
FFN_HIDDEN = ((8 * D_MODEL + 3 * 256 - 1) // (3 * 256)) * 256
IN_SIZES = (GROUP_WIDTH, GROUP_WIDTH,
            GROUP_WIDTH, GROUP_WIDTH, GROUP_WIDTH,
            CONV_CH, CONV_CH,
            GROUP_WIDTH, KV_WIDTH, KV_WIDTH)
IN_WIDTH = sum(IN_SIZES)
RMS_EPS = 1e-6
LN_EPS = 1e-5

kernel_name = "hymba_style_hybrid_encoder_block"


def rms_norm(x, g):
    xf = x.astype(jnp.float32)
    y = xf * lax.rsqrt(jnp.mean(xf * xf, axis=-1, keepdims=True) + RMS_EPS)
    return (y * g.astype(jnp.float32)).astype(x.dtype)


def layer_norm_stats(x):
    xf = x.astype(jnp.float32)
    mu = jnp.mean(xf, axis=-1, keepdims=True)
    xc = xf - mu
    return xc * lax.rsqrt(jnp.mean(xc * xc, axis=-1, keepdims=True) + LN_EPS)


def split_heads(t):
    return t.reshape(t.shape[0], t.shape[1], -1, HEAD_DIM)


def t5_buckets(rel):
    nb = REL_BUCKETS // 2
    max_exact = nb // 2
    ret = jnp.where(rel > 0, nb, 0)
    n = jnp.abs(rel)
    nf = jnp.maximum(n, 1).astype(jnp.float32)
    large = max_exact + (jnp.log(nf / max_exact) / math.log(REL_MAX_DIST / max_exact)
                         * (nb - max_exact)).astype(jnp.int32)
    large = jnp.minimum(large, nb - 1)
    return ret + jnp.where(n < max_exact, n, large)


def sgu_branch(u, v, w_s, b_s):
    bn, s, w = u.shape
    nc = s // SGU_CHUNK
    u = jax.nn.gelu(u)
    v = jax.nn.gelu(v).reshape(bn, nc, SGU_CHUNK, SGU_GROUPS, HEAD_DIM)
    vn = layer_norm_stats(v).astype(u.dtype)
    mixed = jnp.einsum('gpq,bcqgd->bcpgd', w_s, vn) + b_s.T[None, None, :, :, None]
    return u * mixed.reshape(bn, s, w)


def dilated_pattern(q, k, v, rel_table, window, dil):
    bn, s, h, dh = q.shape
    half = window // (2 * dil)
    blk = half
    L = s // dil
    nb = -(-L // blk)
    lp = nb * blk

    def to_sub(t):
        return t.reshape(bn, L, dil, h, dh).transpose(0, 2, 1, 3, 4)

    qs = jnp.pad(to_sub(q), ((0, 0), (0, 0), (0, lp - L), (0, 0), (0, 0)))
    qs = qs.reshape(bn, dil, nb, blk, h, dh)
    pad_kv = ((0, 0), (0, 0), (blk, lp - L + blk), (0, 0), (0, 0))

    def band(t):
        t = jnp.pad(to_sub(t), pad_kv).reshape(bn, dil, nb + 2, blk, h, dh)
        return jnp.concatenate([t[:, :, :-2], t[:, :, 1:-1], t[:, :, 2:]], axis=3)

    kb, vb = band(k), band(v)
    sc = jnp.einsum('brnqhd,brnkhd->brnhqk', qs, kb, preferred_element_type=jnp.float32)
    off = jnp.arange(3 * blk)[None, :] - blk - jnp.arange(blk)[:, None]
    key_idx = jnp.arange(nb)[:, None] * blk - blk + jnp.arange(3 * blk)[None, :]
    valid = (jnp.abs(off) <= half)[None] & ((key_idx >= 0) & (key_idx < L))[:, None, :]
    bias = rel_table[t5_buckets(off * dil)].astype(jnp.float32).transpose(2, 0, 1)
    sc = sc + bias[None, None, None]
    sc = jnp.where(valid[None, None, :, None], sc, -1e30)
    lse = jax.nn.logsumexp(sc, axis=-1)
    p = jnp.exp(sc - lse[..., None])
    o = jnp.einsum('brnhqk,brnkhd->brnqhd', p.astype(v.dtype), vb)
    o = o.reshape(bn, dil, lp, h, dh)[:, :, :L].transpose(0, 2, 1, 3, 4).reshape(bn, s, h, dh)
    lse = lse.transpose(0, 1, 2, 4, 3).reshape(bn, dil, lp, h)[:, :, :L]
    lse = lse.transpose(0, 2, 1, 3).reshape(bn, s, h)
    return o, lse


def dilated_mixture(q, k, v, rel_table):
    outs, lses = [], []
    for window, dil in DIL_PATTERNS:
        o, lse = dilated_pattern(q, k, v, rel_table, window, dil)
        outs.append(o)
        lses.append(lse)
    w = jax.nn.softmax(jnp.stack(lses, axis=0), axis=0)
    return jnp.einsum('gbsh,gbshd->bshd', w.astype(v.dtype), jnp.stack(outs, axis=0))


def conv_branch(a, gate, w_dw, b_dw, ln_g, ln_b):
    hdn = a * jax.nn.sigmoid(gate)
    pad = CONV_WIDTH // 2
    hdn = lax.conv_general_dilated(hdn, w_dw[:, None, :], (1,), [(pad, pad)],
                                   dimension_numbers=('NWC', 'WIO', 'NWC'),
                                   feature_group_count=hdn.shape[-1]) + b_dw
    hdn = (layer_norm_stats(hdn) * ln_g.astype(jnp.float32) + ln_b.astype(jnp.float32)).astype(a.dtype)
    return jax.nn.silu(hdn)


def rope_axis(x, pos):
    half = x.shape[-1] // 2
    freqs = ROPE_THETA ** (-jnp.arange(half, dtype=jnp.float32) / half)
    ang = pos.astype(jnp.float32)[:, None] * freqs[None, :]
    cos = jnp.cos(ang)[:, None, :]
    sin = jnp.sin(ang)[:, None, :]
    xf = x.astype(jnp.float32)
    x1, x2 = xf[..., :half], xf[..., half:]
    return jnp.concatenate([x1 * cos - x2 * sin, x2 * cos + x1 * sin], axis=-1).astype(x.dtype)


def axial_rope(x, row, col):
    d2 = x.shape[-1] // 2
    return jnp.concatenate([rope_axis(x[..., :d2], row), rope_axis(x[..., d2:], col)], axis=-1)


def gqa_branch(q, k, v):
    bn, s, hq, dh = q.shape
    hkv = k.shape[2]
    g = hq // hkv
    nq = s // Q_BLOCK
    qb = q.reshape(bn, nq, Q_BLOCK, hkv, g, dh).transpose(1, 0, 2, 3, 4, 5)

    def block(qblk):
        sc = jnp.einsum('bqhgd,bkhd->bhgqk', qblk, k, preferred_element_type=jnp.float32)
        p = jax.nn.softmax(sc, axis=-1)
        return jnp.einsum('bhgqk,bkhd->bqhgd', p.astype(v.dtype), v)

    o = lax.map(block, qb)
    return o.transpose(1, 0, 2, 3, 4, 5).reshape(bn, s, hq, dh)


def setup_inputs(seed: int = 0) -> dict:
    key = jax.random.key(seed)
    ks = jax.random.split(key, 20)
    f32 = jnp.float32
    nrm = lambda k, shape, scale: jax.random.normal(k, shape, f32) * scale
    gain = lambda k, shape: 1.0 + 0.02 * jax.random.normal(k, shape, f32)
    return {
        "x": jax.random.normal(ks[0], (BATCH, SEQ, D_MODEL), f32),
        "rel_bias": nrm(ks[1], (REL_BUCKETS, DIL_HEADS), 0.5),
        "norm1_g": gain(ks[2], (DEPTH, D_MODEL)),
        "w_in": nrm(ks[3], (DEPTH, D_MODEL, IN_WIDTH), D_MODEL ** -0.5),
        "sgu_w": nrm(ks[4], (DEPTH, SGU_GROUPS, SGU_CHUNK, SGU_CHUNK), SGU_CHUNK ** -0.5),
        "sgu_b": gain(ks[5], (DEPTH, SGU_GROUPS, SGU_CHUNK)),
        "dil_qn_g": gain(ks[6], (DEPTH, HEAD_DIM)),
        "dil_kn_g": gain(ks[7], (DEPTH, HEAD_DIM)),
        "conv_w": nrm(ks[8], (DEPTH, CONV_WIDTH, CONV_CH), CONV_WIDTH ** -0.5),
        "conv_b": nrm(ks[9], (DEPTH, CONV_CH), 0.02),
        "conv_ln_g": gain(ks[10], (DEPTH, CONV_CH)),
        "conv_ln_b": nrm(ks[11], (DEPTH, CONV_CH), 0.02),
        "gqa_qn_g": gain(ks[12], (DEPTH, HEAD_DIM)),
        "gqa_kn_g": gain(ks[13], (DEPTH, HEAD_DIM)),
        "mix_norm_g": gain(ks[14], (DEPTH, MIX_WIDTH)),
        "w_out": nrm(ks[15], (DEPTH, MIX_WIDTH, D_MODEL), MIX_WIDTH ** -0.5),
        "norm2_g": gain(ks[16], (DEPTH, D_MODEL)),
        "w_gate": nrm(ks[17], (DEPTH, D_MODEL, FFN_HIDDEN), D_MODEL ** -0.5),
        "w_up": nrm(ks[18], (DEPTH, D_MODEL, FFN_HIDDEN), D_MODEL ** -0.5),
        "w_down": nrm(ks[19], (DEPTH, FFN_HIDDEN, D_MODEL), FFN_HIDDEN ** -0.5),
    }


def reference(x, rel_bias, norm1_g, w_in, sgu_w, sgu_b, dil_qn_g, dil_kn_g, conv_w, conv_b,
              conv_ln_g, conv_ln_b, gqa_qn_g, gqa_kn_g, mix_norm_g, w_out, norm2_g,
              w_gate, w_up, w_down):
    bn, s, _ = x.shape
    rows = s // GRID_W
    row = jnp.repeat(jnp.arange(rows), GRID_W)
    col = jnp.tile(jnp.arange(GRID_W), rows)
    split_at = np.cumsum(IN_SIZES)[:-1].tolist()
    scale = HEAD_DIM ** -0.5
    for l in range(DEPTH):
        h = rms_norm(x, norm1_g[l])
        z = h @ w_in[l]
        a_u, a_v, b_q, b_k, b_v, c_a, c_g, d_q, d_k, d_v = jnp.split(z, split_at, axis=-1)
        y_a = sgu_branch(a_u, a_v, sgu_w[l], sgu_b[l])
        qb = rms_norm(split_heads(b_q), dil_qn_g[l]) * scale
        kb = rms_norm(split_heads(b_k), dil_kn_g[l])
        y_b = dilated_mixture(qb, kb, split_heads(b_v), rel_bias).reshape(bn, s, GROUP_WIDTH)
        y_c = conv_branch(c_a, c_g, conv_w[l], conv_b[l], conv_ln_g[l], conv_ln_b[l])
        qd = axial_rope(rms_norm(split_heads(d_q), gqa_qn_g[l]), row, col) * scale
        kd = axial_rope(rms_norm(split_heads(d_k), gqa_kn_g[l]), row, col)
        y_d = gqa_branch(qd, kd, split_heads(d_v)).reshape(bn, s, GROUP_WIDTH)
        y = jnp.stack([y_a, y_b, y_c, y_d], axis=2)
        y = rms_norm(y, mix_norm_g[l].reshape(N_MIXERS, GROUP_WIDTH)).reshape(bn, s, MIX_WIDTH)
        x = x + y @ w_out[l]
        h = rms_norm(x, norm2_g[l])
        x = x + (jax.nn.silu(h @ w_gate[l]) * (h @ w_up[l])) @ w_down[l]
    return x
```

```python
import math
from contextlib import ExitStack

import numpy as np
import concourse.bass as bass
import concourse.mybir as mybir
from concourse.bass_utils import run_bass_kernel_spmd

F32 = mybir.dt.float32
BF16 = mybir.dt.bfloat16
AF = mybir.ActivationFunctionType
ALU = mybir.AluOpType
AX = mybir.AxisListType

D = 2048
S = 2048
DEPTH = 2
NSEQ = 2
INW = 4352
FFN = 5632
HD = 64
RMS_EPS = 1e-6
LN_EPS = 1e-5
NSLOT = 4

DBG = {"layers": DEPTH, "nseq": NSEQ, "stop_after": None, "dump": False, "ncores": 8}


class Res:
    __slots__ = ("name", "w", "r", "sem", "dcount", "key")

    def __init__(self, name):
        self.name = name
        self.w = {}
        self.r = {}
        self.sem = None
        self.dcount = 0
        self.key = "d:" + name


class Eng:
    def __init__(self, name, eng, sem):
        self.name = name
        self.eng = eng
        self.sem = sem
        self.count = 0
        self.waited = {}


class KB:
    def __init__(self, nc, es):
        self.nc = nc
        self.es = es
        self.E = {}
        for name, eng in (("pe", nc.tensor), ("act", nc.scalar), ("dve", nc.vector), ("pool", nc.gpsimd), ("sp", nc.sync)):
            sem = es.enter_context(nc.semaphore("sem_" + name)) if name in ("pe", "act", "dve") else None
            self.E[name] = Eng(name, eng, sem)
        self.dsem = {}
        self.local = []
        self.sem_pool = []
        self.semh = {}
        self.semtot = {}
        self.nres = 0

    def res(self, name, local=True):
        self.nres += 1
        r = Res(f"{name}_{self.nres}")
        if local:
            self.local.append(r)
        return r

    def _dma_sem(self, r):
        if r.sem is None:
            if self.sem_pool:
                r.sem, r.key, r.dcount = self.sem_pool.pop()
            else:
                r.sem = self.es.enter_context(self.nc.semaphore("ds_" + r.name))
            self.dsem[r.key] = r
            self.semh[r.key] = r.sem
            self.semtot[r.key] = r.dcount
        return r.sem

    def wait(self, eng, key, val):
        E = self.E[eng]
        if key == eng and eng == "pe":
            return
        if key in self.dsem:
            val = self.semtot[key]
            sem = self.semh[key]
        else:
            sem = self.E[key].sem
        if E.waited.get(key, 0) >= val:
            return
        E.eng.wait_ge(sem, val)
        E.waited[key] = val

    def _deps(self, eng, reads, writes, more):
        for r in reads:
            for k, v in r.w.items():
                self.wait(eng, k, v)
        for w in writes:
            for k, v in w.w.items():
                self.wait(eng, k, v)
            for k, v in w.r.items():
                self.wait(eng, k, v)
        for w in more:
            for k, v in w.r.items():
                self.wait(eng, k, v)

    def _record(self, key, val, reads, writes, more):
        for r in reads:
            if r.r.get(key, 0) < val:
                r.r[key] = val
        for w in writes:
            w.w = {key: val}
            w.r = {}
        for w in more:
            w.w[key] = val

    def op(self, eng, fn, reads=(), writes=(), more=(), inc=True):
        E = self.E[eng]
        self._deps(eng, reads, writes, more)
        ins = fn(E.eng)
        val = E.count + 1
        if inc:
            ins.then_inc(E.sem, 1)
            E.count = val
        self._record(eng, val, reads, writes, more)
        return ins

    def dma(self, q, out, in_, reads=(), writes=(), more=(), semres=None, **kw):
        E = self.E[q]
        self._deps(q, reads, writes, more)
        sem = self._dma_sem(semres)
        ins = E.eng.dma_start(out=out, in_=in_, **kw)
        ins.then_inc(sem, 16)
        semres.dcount += 16
        self.semtot[semres.key] = semres.dcount
        self._record(semres.key, semres.dcount, reads, writes, more)

    def barrier(self):
        evs = {}
        for r in self.local:
            for d in (r.w, r.r):
                for k, v in d.items():
                    if k in self.dsem:
                        evs[k] = max(evs.get(k, 0), v)
        for x in ("pe", "act", "dve", "pool", "sp"):
            for f in ("pe", "act", "dve"):
                self.wait(x, f, self.E[f].count)
            for k, v in evs.items():
                self.wait(x, k, v)
        for r in self.local:
            if r.sem is not None:
                self.sem_pool.append((r.sem, r.key, r.dcount))
                r.sem = None
        self.local = []


def _bc(ap, shape):
    return ap.broadcast_to(shape)


def build_program():
    nc = bass.Bass("TRN2", target_bir_lowering=False)
    L = DBG["layers"]
    NS = DBG["nseq"]
    dump = DBG["dump"]
    skind = "ExternalOutput" if dump else "Internal"

    def din(name, shape, dt=F32):
        return nc.dram_tensor(name, list(shape), dt, kind="ExternalInput")

    x_in = din("x", [NSEQ, S, D])
    norm1_g = din("norm1_g", [DEPTH, D]); w_in = din("w_in", [DEPTH, D, INW])
    sgu_w = din("sgu_w", [DEPTH, 8, 128, 128]); sgu_b = din("sgu_b", [DEPTH, 8, 128])
    dil_qn_g = din("dil_qn_g", [DEPTH, HD]); dil_kn_g = din("dil_kn_g", [DEPTH, HD])
    conv_w = din("conv_w", [DEPTH, 31, 512]); conv_b = din("conv_b", [DEPTH, 512])
    conv_ln_g = din("conv_ln_g", [DEPTH, 512]); conv_ln_b = din("conv_ln_b", [DEPTH, 512])
    gqa_qn_g = din("gqa_qn_g", [DEPTH, HD]); gqa_kn_g = din("gqa_kn_g", [DEPTH, HD])
    mix_norm_g = din("mix_norm_g", [DEPTH, D]); w_out = din("w_out", [DEPTH, D, D])
    norm2_g = din("norm2_g", [DEPTH, D])
    w_gate = din("w_gate", [DEPTH, D, FFN]); w_up = din("w_up", [DEPTH, D, FFN]); w_down = din("w_down", [DEPTH, FFN, D])
    c_ident = din("c_ident", [128, 128]); c_anti = din("c_anti", [128, 128])
    c_cos = din("c_cos", [S, HD]); c_sin = din("c_sin", [S, HD])
    c_rbg = din("c_rbg", [8, 4096]); c_lmult = din("c_lmult", [8, 4096])

    out = nc.dram_tensor("out", [NSEQ, S, D], F32, kind="ExternalOutput")

    def scr(name, shape, dt):
        return [nc.dram_tensor(f"{name}{s}", list(shape), dt, kind=skind) for s in range(NSEQ)]

    xres = scr("xres", [S, D], F32)
    qTB = scr("qTB", [4, 128, S], BF16); kTB = scr("kTB", [4, 128, S], BF16); vB = scr("vB", [16, 128, 520], BF16)
    qTD = scr("qTD", [4, 128, S], BF16); kTD = scr("kTD", [2, 128, S], BF16); vD = scr("vD", [16, 128, 130], BF16)
    gluT = scr("gluT", [4, 128, S], F32)
    yT = scr("yT", [16, 128, S], BF16)
    biasR = nc.dram_tensor("biasR", [8, 4096], F32, kind=skind)
    wsrc = {"w_in": w_in, "w_out": w_out, "w_gate": w_gate, "w_up": w_up, "w_down": w_down}
    wb = {nm: nc.dram_tensor("wb_" + nm, list(t_.shape), BF16, kind="Internal") for nm, t_ in wsrc.items()}

    with ExitStack() as es:
        K = KB(nc, es)

        sbn = [0]

        def SB(st, name, shape, dt=F32):
            sbn[0] += 1
            return st.enter_context(nc.sbuf_tensor(f"{name}_{sbn[0]}", list(shape), dt))

        banks = [es.enter_context(nc.psum_tensor(f"bank{i}", [128, 512], F32)) for i in range(8)]
        bank_res = [K.res(f"bank{i}", local=False) for i in range(8)]
        ring = {"i": 0, "n": 6}

        def ps_next():
            i = ring["i"] % ring["n"]
            ring["i"] += 1
            return banks[i], bank_res[i]

        identb = SB(es, "identb", [128, 128], BF16); antib = SB(es, "antib", [128, 128], BF16)
        identf = SB(es, "identf", [128, 128], F32); onesf = SB(es, "onesf", [128, 128], F32)
        epsr = SB(es, "epsr", [128, 2], F32)
        gconst = K.res("gconst", local=False)
        K.dma("pool", identb[:], c_ident.ap(), writes=[gconst], semres=gconst)
        K.dma("pool", antib[:], c_anti.ap(), more=[gconst], semres=gconst)
        gconst3 = K.res("gconst3", local=False)
        K.dma("sp", identf[:], c_ident.ap(), writes=[gconst3], semres=gconst3)
        gconst2 = K.res("gconst2", local=False)
        K.op("dve", lambda e: e.memset(onesf[:], 1.0), writes=[gconst2])
        K.op("dve", lambda e: e.memset(epsr[:, 0:1], RMS_EPS), more=[gconst2])
        K.op("dve", lambda e: e.memset(epsr[:, 1:2], LN_EPS), more=[gconst2])
        GC = [gconst, gconst2, gconst3]

        wslots = [SB(es, f"wslot{i}", [128, 16, 512], BF16) for i in range(NSLOT)]
        wres = [K.res(f"wslot{i}", local=False) for i in range(NSLOT)]

        R = {}
        for nm in ("xres", "qTB", "kTB", "vB", "qTD", "kTD", "vD", "gluT", "yT"):
            R[nm] = [K.res(f"{nm}{s}", local=False) for s in range(NSEQ)]
        R["biasR"] = K.res("biasR", local=False)

        sched = []
        for l in range(L):
            for s in range(NS):
                for t in range(4):
                    for c in range(8):
                        sched.append(("w_in", l, 0, 16, c * 512, 512))
                    sched.append(("w_in", l, 0, 16, 4096, 256))
                for t in range(4):
                    for n in range(4):
                        sched.append(("w_out", l, 0, 16, n * 512, 512))
                    for mg in range(11):
                        sched.append(("w_gate", l, 0, 16, mg * 512, 512))
                        sched.append(("w_up", l, 0, 16, mg * 512, 512))
                    for n in range(4):
                        for (k0, nk) in ((0, 16), (16, 16), (32, 12)):
                            sched.append(("w_down", l, k0, nk, n * 512, 512))
        ws = {"issued": 0, "next": 0}
        wbres = {}
        cpieces = []
        for l in range(L):
            for nm in ("w_in", "w_out", "w_gate", "w_up", "w_down"):
                wbres[(nm, l)] = K.res(f"wb_{nm}{l}", local=False)
                rows = wsrc[nm].shape[1]
                for r0 in range(0, rows, 256):
                    cpieces.append((nm, l, r0, min(rows, r0 + 256)))
        cstate = {"i": 0, "first": set()}

        def conv_issue(n):
            for _ in range(n):
                i = cstate["i"]
                if i >= len(cpieces):
                    return
                cstate["i"] += 1
                nm, l, r0, r1 = cpieces[i]
                r_ = wbres[(nm, l)]
                fw_ = (nm, l) not in cstate["first"]
                cstate["first"].add((nm, l))
                K.dma("pool", wb[nm].ap()[l, r0:r1, :], wsrc[nm].ap()[l, r0:r1, :], writes=[r_] if fw_ else (), more=() if fw_ else [r_], semres=r_)

        def conv_until(nm, l):
            while any(p_[0] == nm and p_[1] == l for p_ in cpieces[cstate["i"]:]):
                conv_issue(1)

        def w_issue():
            i = ws["issued"]
            if i >= len(sched):
                return
            ws["issued"] += 1
            nm, l, k0, nk, c0, ncols = sched[i]
            conv_until(nm, l)
            slot = i % NSLOT
            src = wb[nm].ap()[l, k0 * 128:(k0 + nk) * 128, c0:c0 + ncols].rearrange("(k p) n -> p k n", p=128)
            first = True
            for q0 in range(0, nk, 8):
                q1 = min(nk, q0 + 8)
                if first:
                    K.dma("pool", wslots[slot][:, q0:q1, 0:ncols], src[:, q0:q1, :], reads=[wbres[(nm, l)]], writes=[wres[slot]], semres=wres[slot])
                    first = False
                else:
                    K.dma("pool", wslots[slot][:, q0:q1, 0:ncols], src[:, q0:q1, :], reads=[wbres[(nm, l)]], more=[wres[slot]], semres=wres[slot])
            conv_issue(2)

        def w_get(expect=None):
            i = ws["next"]
            ws["next"] += 1
            assert i < ws["issued"]
            if expect is not None:
                assert sched[i][0] == expect[0] and sched[i][2:] == expect[1:], (sched[i], expect)
            return wslots[i % NSLOT], wres[i % NSLOT]

        def w_done():
            w_issue()

        for _ in range(NSLOT):
            w_issue()

        dbg_n = [0]

        def dbgdump(name, ap, res, shape, dt=F32):
            if not dump:
                return
            t_ = nc.dram_tensor("dbg_" + name, list(shape), dt, kind="ExternalOutput")
            r_ = K.res("dbg_" + name, local=False)
            K.dma("sp", t_.ap(), ap, reads=[res] if not isinstance(res, list) else res, writes=[r_], semres=r_)

        def rstd_from_sum(st, ss_ap, n, inv, eps_col, res_in, tmp, tmp_res, out_ap, out_res):
            K.op("dve", lambda e: e.tensor_scalar(out=tmp, in0=ss_ap, scalar1=inv, scalar2=epsr[:, eps_col:eps_col + 1],
                                                  op0=ALU.mult, op1=ALU.add), reads=[res_in] + GC, writes=[tmp_res])
            K.op("act", lambda e: e.activation(out=tmp, in_=tmp, func=AF.Sqrt), reads=[tmp_res], writes=[tmp_res])
            K.op("dve", lambda e: e.reciprocal(out=out_ap, in_=tmp), reads=[tmp_res], writes=[out_res])

        def transpose_blocks(src_ap_fn, nblk, src_res, dst_ap_fn, dst_res, first_write, ident=None, evac="act"):
            idm = identb if ident is None else ident
            pb, pr = ps_next()
            for c in range(nblk):
                K.op("pe", lambda e, c=c: e.matmul(pb[:, c * 128:(c + 1) * 128], lhsT=src_ap_fn(c), rhs=idm[:], start=True, stop=True),
                     reads=[src_res] + GC, writes=[pr] if c == 0 else (), more=() if c == 0 else [pr], inc=(c == nblk - 1))
            dst = dst_ap_fn()
            srcv = pb[:, 0:nblk * 128].rearrange("p (c n) -> p c n", c=nblk)
            if evac == "act":
                K.op("act", lambda e: e.activation(out=dst, in_=srcv, func=AF.Copy), reads=[pr],
                     writes=[dst_res] if first_write else (), more=() if first_write else [dst_res])
            else:
                K.op("dve", lambda e: e.tensor_copy(out=dst, in_=srcv), reads=[pr],
                     writes=[dst_res] if first_write else (), more=() if first_write else [dst_res])

        with ExitStack() as st:
            ta = SB(st, "bt_a", [8, 4096]); tb = SB(st, "bt_b", [8, 4096])
            ra = K.res("bt_a"); rb = K.res("bt_b")
            K.dma("sp", ta[:], c_rbg.ap(), writes=[ra], semres=ra)
            K.dma("sp", tb[:], c_lmult.ap(), writes=[rb], semres=rb)
            K.op("dve", lambda e: e.tensor_tensor(out=ta[:], in0=ta[:], in1=tb[:], op=ALU.add), reads=[rb], writes=[ra])
            K.dma("sp", biasR.ap(), ta[:], reads=[ra], writes=[R["biasR"]], semres=R["biasR"])
            K.barrier()

        def phase1(l, s):
            src_x = x_in.ap()[s] if l == 0 else xres[s].ap()
            with ExitStack() as st:
                cres = K.res("p1const")
                g1bc = SB(st, "g1bc", [128, D]); gqB = SB(st, "gqB", [128, 512]); gkB = SB(st, "gkB", [128, 512])
                gqD = SB(st, "gqD", [128, 512]); gkD = SB(st, "gkD", [128, 128]); gainA = SB(st, "gainA", [128, 512])
                cosT = SB(st, "cosT", [128, 16, HD]); sinT = SB(st, "sinT", [128, 16, HD])
                wsraw = SB(st, "wsraw", [128, 8, 128]); WsT = SB(st, "WsT", [128, 8, 128], BF16); bsT = SB(st, "bsT", [128, 8])
                first = [True]

                def cdma(dst, src, **kw):
                    if first[0]:
                        K.dma("sp", dst, src, writes=[cres], semres=cres, **kw); first[0] = False
                    else:
                        K.dma("sp", dst, src, more=[cres], semres=cres, **kw)
                cdma(g1bc[:], bass.AP(norm1_g, l * D, [[0, 128], [1, D]]))
                cdma(gqB[:].rearrange("p (h d) -> p h d", h=8), bass.AP(dil_qn_g, l * HD, [[0, 128], [0, 8], [1, HD]]))
                cdma(gkB[:].rearrange("p (h d) -> p h d", h=8), bass.AP(dil_kn_g, l * HD, [[0, 128], [0, 8], [1, HD]]))
                cdma(gqD[:].rearrange("p (h d) -> p h d", h=8), bass.AP(gqa_qn_g, l * HD, [[0, 128], [0, 8], [1, HD]]))
                cdma(gkD[:].rearrange("p (h d) -> p h d", h=2), bass.AP(gqa_kn_g, l * HD, [[0, 128], [0, 2], [1, HD]]))
                cdma(gainA[:], bass.AP(mix_norm_g, l * D, [[0, 128], [1, 512]]))
                cdma(cosT[:], c_cos.ap().rearrange("(b p) d -> p b d", p=128))
                cdma(sinT[:], c_sin.ap().rearrange("(b p) d -> p b d", p=128))
                cdma(wsraw[:], sgu_w.ap()[l].rearrange("g p q -> p g q"))
                cdma(bsT[:], sgu_b.ap()[l].rearrange("g p -> p g"), allow_slow_non_contiguous=True)
                c2 = K.res("p1const2")
                K.op("dve", lambda e: e.tensor_scalar(out=gqB[:], in0=gqB[:], scalar1=0.125, scalar2=None, op0=ALU.mult), reads=[cres], writes=[c2])
                K.op("dve", lambda e: e.tensor_scalar(out=gqD[:], in0=gqD[:], scalar1=0.125, scalar2=None, op0=ALU.mult), more=[c2])
                for g0 in (0, 4):
                    pb, pr = ps_next()
                    for g in range(g0, g0 + 4):
                        K.op("pe", lambda e, g=g: e.matmul(pb[:, (g - g0) * 128:(g - g0 + 1) * 128], lhsT=wsraw[:, g, :], rhs=identf[:], start=True, stop=True),
                             reads=[cres] + GC, writes=[pr] if g == g0 else (), more=() if g == g0 else [pr], inc=(g == g0 + 3))
                    K.op("act", lambda e: e.activation(out=WsT[:, g0:g0 + 4, :], in_=pb[:].rearrange("p (g n) -> p g n", g=4), func=AF.Copy),
                         reads=[pr], more=[c2])
                CR = [cres, c2] + GC

                xb = [SB(st, f"xb{i}", [128, D]) for i in range(2)]; xbr = [K.res(f"xb{i}") for i in range(2)]
                junk = SB(st, "junk", [128, D], BF16); junkr = K.res("junk")
                hb = [SB(st, f"hb{i}", [128, D], BF16) for i in range(2)]; hbr = [K.res(f"hb{i}") for i in range(2)]
                hT = SB(st, "hT", [128, 16, 512], BF16); hTr = K.res("hT")
                sml = SB(st, "sml", [128, 64]); smlr = [K.res(f"sml{i}") for i in range(8)]
                qTBs = SB(st, "qTBs", [128, 4, 512], BF16); kTBs = SB(st, "kTBs", [128, 4, 512], BF16)
                vBs = SB(st, "vBs", [128, 4, 520], BF16); qTDs = SB(st, "qTDs", [128, 4, 512], BF16)
                kTDs = SB(st, "kTDs", [128, 2, 512], BF16); vDs = SB(st, "vDs", [128, 4, 130], BF16)
                yTAs = SB(st, "yTAs", [128, 4, 512], BF16); glus = SB(st, "glus", [128, 4, 512]); ca = SB(st, "ca", [128, 4, 512])
                r_qTBs, r_kTBs, r_vBs, r_qTDs, r_kTDs, r_vDs, r_yTAs, r_glus, r_ca = [K.res(n) for n in
                    ("qTBs", "kTBs", "vBs", "qTDs", "kTDs", "vDs", "yTAs", "glus", "ca")]
                ug = SB(st, "ug", [128, 4, 512]); ugr = [K.res(f"ug{i}") for i in range(4)]
                NT = 5
                tmp = [SB(st, f"tmp{i}", [128, 512]) for i in range(NT)]; tmpr = [K.res(f"tmp{i}") for i in range(NT)]
                NTB = 6
                tbf = [SB(st, f"tbf{i}", [128, 512], BF16) for i in range(NTB)]; tbfr = [K.res(f"tbf{i}") for i in range(NTB)]
                tcnt = {"t": 0, "b": 0}

                def T():
                    i = tcnt["t"] % NT; tcnt["t"] += 1
                    return tmp[i], tmpr[i]

                def TB():
                    i = tcnt["b"] % NTB; tcnt["b"] += 1
                    return tbf[i], tbfr[i]
                K.op("dve", lambda e: e.memset(vBs[:], 1.0), writes=[r_vBs])
                K.op("dve", lambda e: e.memset(vDs[:], 1.0), writes=[r_vDs])

                def group_rstd(src_ap, src_res, ng, eps_col, slot):
                    sq, sqr = T()
                    K.op("act", lambda e: e.activation(out=sq[:, 0:ng * 64], in_=src_ap, func=AF.Square), reads=[src_res], writes=[sqr])
                    o = slot * 8
                    K.op("dve", lambda e: e.tensor_reduce(out=sml[:, o:o + ng], in_=sq[:, 0:ng * 64].rearrange("p (g d) -> p g d", g=ng), axis=AX.X, op=ALU.add),
                         reads=[sqr], writes=[smlr[slot]])
                    rstd_from_sum(st, sml[:, o:o + ng], ng, 1.0 / 64, eps_col, smlr[slot], sml[:, o:o + ng], smlr[slot], sml[:, o:o + ng], smlr[slot])
                    return sml[:, o:o + ng]

                def row_rstd(src_ap, src_res, n, slot):
                    o = slot * 8
                    K.op("act", lambda e: e.activation(out=junk[:, 0:n], in_=src_ap, func=AF.Square, accum_out=sml[:, o:o + 1]),
                         reads=[src_res], writes=[junkr, smlr[slot]])
                    rstd_from_sum(st, sml[:, o:o + 1], 1, 1.0 / n, 0, smlr[slot], sml[:, o + 1:o + 2], smlr[slot], sml[:, o + 2:o + 3], smlr[slot])
                    return sml[:, o + 2:o + 3]

                def rope(src, src_res, nh, bi, dst, dst_res):
                    t1, t1r = T(); t2, t2r = T()
                    w = nh * 64
                    cosb = _bc(cosT[:, bi, :].unsqueeze(1), [128, nh, HD])
                    K.op("dve", lambda e: e.tensor_tensor(out=t1[:, 0:w].rearrange("p (h d) -> p h d", h=nh), in0=src.rearrange("p (h d) -> p h d", h=nh), in1=cosb, op=ALU.mult),
                         reads=[src_res] + CR, writes=[t1r])
                    sv = src.rearrange("p (h a f i) -> p h a f i", h=nh, a=2, f=2)
                    tv = t2[:, 0:w].rearrange("p (h a f i) -> p h a f i", h=nh, a=2, f=2)
                    sn = sinT[:, bi, :].rearrange("p (a f i) -> p a f i", a=2, f=2)
                    for f in (0, 1):
                        sb_ = _bc(sn[:, :, f, :].unsqueeze(1), [128, nh, 2, 16])
                        K.op("dve", lambda e, f=f, sb_=sb_: e.tensor_tensor(out=tv[:, :, :, f, :], in0=sv[:, :, :, 1 - f, :], in1=sb_, op=ALU.mult),
                             reads=[src_res] + CR, writes=[t2r] if f == 0 else (), more=() if f == 0 else [t2r])
                    K.op("dve", lambda e: e.tensor_tensor(out=dst, in0=t1[:, 0:w], in1=t2[:, 0:w], op=ALU.add), reads=[t1r, t2r], writes=[dst_res])

                live = []
                GAP = 3

                def tick():
                    for g_ in list(live):
                        try:
                            next(g_)
                        except StopIteration:
                            live.remove(g_)
                ring["n"] = 8
                for t in range(4):
                    def stageA(t, b):
                        bi = t * 4 + b
                        xs_, xr_ = xb[bi % 2], xbr[bi % 2]
                        K.dma("sp", xs_[:], src_x[bi * 128:(bi + 1) * 128, :], reads=[R["xres"][s]] if l > 0 else (), writes=[xr_], semres=xr_)
                        rs = row_rstd(xs_[:], xr_, D, 0)
                        K.op("dve", lambda e: e.scalar_tensor_tensor(out=hb[b % 2][:], in0=xs_[:], scalar=rs, in1=g1bc[:], op0=ALU.mult, op1=ALU.mult),
                             reads=[xr_, smlr[0]] + CR, writes=[hbr[b % 2]])

                    def stageB(b):
                        for k0 in range(0, 16, 4):
                            transpose_blocks(lambda c, k0=k0: hb[b % 2][:, (k0 + c) * 128:(k0 + c + 1) * 128], 4, hbr[b % 2],
                                             lambda k0=k0, b=b: hT[:, k0:k0 + 4, b * 128:(b + 1) * 128], hTr, first_write=(b == 0 and k0 == 0),
                                             evac="act" if (k0 // 4) % 2 == 0 else "dve")
                    if t == 0:
                        stageA(0, 0)
                        stageA(0, 1)
                    stageB(0)
                    stageA(t, 2)
                    stageB(1)
                    stageA(t, 3)
                    stageB(2)
                    stageB(3)
                    for c in range(9):
                        ncols = 512 if c < 8 else 256
                        wsl, wr = w_get(("w_in", 0, 16, c * 512, ncols))
                        if c in (5, 6):
                            for cc in range(4):
                                pb, pr = ps_next()
                                for k in range(16):
                                    K.op("pe", lambda e, k=k, cc=cc: e.matmul(pb[:], lhsT=wsl[:, k, cc * 128:(cc + 1) * 128], rhs=hT[:, k, :], start=(k == 0), stop=(k == 15)),
                                         reads=[wr, hTr], writes=[pr] if k == 0 else (), more=() if k == 0 else [pr], inc=(k == 15))
                                if c == 5:
                                    K.op("act", lambda e, cc=cc: e.activation(out=ca[:, cc, :], in_=pb[:], func=AF.Copy), reads=[pr],
                                         writes=[r_ca] if cc == 0 else (), more=() if cc == 0 else [r_ca])
                                else:
                                    sg, sgr = T()
                                    K.op("act", lambda e: e.activation(out=sg[:], in_=pb[:], func=AF.Sigmoid), reads=[pr], writes=[sgr])
                                    K.op("dve", lambda e, cc=cc: e.tensor_tensor(out=glus[:, cc, :], in0=ca[:, cc, :], in1=sg[:], op=ALU.mult), reads=[sgr, r_ca],
                                         writes=[r_glus] if cc == 0 else (), more=() if cc == 0 else [r_glus])
                                tick()
                            if c == 6 and t + 1 < 4:
                                stageA(t + 1, 0)
                                stageA(t + 1, 1)
                            if c == 6:
                                K.dma("sp", gluT[s].ap()[:, :, t * 512:(t + 1) * 512].rearrange("c p n -> p c n"), glus[:], reads=[r_glus], writes=[R["gluT"][s]] if t == 0 else (),
                                      more=() if t == 0 else [R["gluT"][s]], semres=R["gluT"][s])
                            w_done()
                            continue
                        for b in range(4):
                            bi = t * 4 + b
                            pb, pr = ps_next()
                            for k in range(16):
                                K.op("pe", lambda e, k=k, b=b: e.matmul(pb[:, 0:ncols], lhsT=hT[:, k, b * 128:(b + 1) * 128], rhs=wsl[:, k, 0:ncols], start=(k == 0), stop=(k == 15)),
                                     reads=[wr, hTr], writes=[pr] if k == 0 else (), more=() if k == 0 else [pr], inc=(k == 15))
                            def post(c=c, b=b, bi=bi, pb=pb, pr=pr):
                                fw = (b == 0)
                                if c == 0:
                                    K.op("act", lambda e, b=b: e.activation(out=ug[:, b, :], in_=pb[:], func=AF.Gelu_apprx_tanh), reads=[pr], writes=[ugr[b]])
                                elif c == 1:
                                    vg, vgr = T()
                                    K.op("act", lambda e: e.activation(out=vg[:], in_=pb[:], func=AF.Gelu_apprx_tanh), reads=[pr], writes=[vgr])
                                    vg3 = vg[:].rearrange("p (g d) -> p g d", g=8)
                                    K.op("dve", lambda e: e.tensor_reduce(out=sml[:, 8:16], in_=vg3, axis=AX.X, op=ALU.add), reads=[vgr], writes=[smlr[1]])
                                    K.op("dve", lambda e: e.tensor_scalar(out=sml[:, 8:16], in0=sml[:, 8:16], scalar1=1.0 / 64, scalar2=None, op0=ALU.mult), reads=[smlr[1]], writes=[smlr[1]])
                                    xc, xcr = T()
                                    K.op("dve", lambda e: e.tensor_tensor(out=xc[:].rearrange("p (g d) -> p g d", g=8), in0=vg3, in1=_bc(sml[:, 8:16].unsqueeze(2), [128, 8, 64]), op=ALU.subtract),
                                         reads=[vgr, smlr[1]], writes=[xcr])
                                    rsg = group_rstd(xc[:], xcr, 8, 1, 2)
                                    vn, vnr = TB()
                                    K.op("dve", lambda e: e.tensor_tensor(out=vn[:].rearrange("p (g d) -> p g d", g=8), in0=xc[:].rearrange("p (g d) -> p g d", g=8),
                                                                           in1=_bc(rsg.unsqueeze(2), [128, 8, 64]), op=ALU.mult), reads=[xcr, smlr[2]], writes=[vnr])
                                    for _ in range(GAP):
                                        yield
                                    p2, p2r = ps_next()
                                    for g in range(8):
                                        K.op("pe", lambda e, g=g: e.matmul(p2[:, g * 64:(g + 1) * 64], lhsT=WsT[:, g, :], rhs=vn[:, g * 64:(g + 1) * 64], start=True, stop=True),
                                             reads=[vnr] + CR, writes=[p2r] if g == 0 else (), more=() if g == 0 else [p2r], inc=(g == 7))
                                    ya, yar = T()
                                    K.op("dve", lambda e: e.tensor_tensor(out=ya[:].rearrange("p (g d) -> p g d", g=8), in0=p2[:].rearrange("p (g d) -> p g d", g=8),
                                                                           in1=_bc(bsT[:].unsqueeze(2), [128, 8, 64]), op=ALU.add), reads=[p2r] + CR, writes=[yar])
                                    K.op("dve", lambda e, b=b: e.tensor_tensor(out=ya[:], in0=ya[:], in1=ug[:, b, :], op=ALU.mult), reads=[ugr[b]], writes=[yar])
                                    if t == 0 and b == 0:
                                        dbgdump("ug", ug[:, 0, :], ugr[0], [128, 512]); dbgdump("vg", vg[:], vgr, [128, 512]); dbgdump("xc", xc[:], xcr, [128, 512])
                                        dbgdump("vn", vn[:], vnr, [128, 512], BF16); dbgdump("ya", ya[:], yar, [128, 512])
                                        dbgdump("WsT", WsT[:], CR, [128, 8, 128], BF16); dbgdump("bsT", bsT[:], CR, [128, 8]); dbgdump("sml", sml[:], smlr[2], [128, 64])
                                    rs = row_rstd(ya[:], yar, 512, 3)
                                    yb_, ybr_ = TB()
                                    K.op("dve", lambda e: e.scalar_tensor_tensor(out=yb_[:], in0=ya[:], scalar=rs, in1=gainA[:], op0=ALU.mult, op1=ALU.mult),
                                         reads=[yar, smlr[3]] + CR, writes=[ybr_])
                                    for _ in range(GAP):
                                        yield
                                    transpose_blocks(lambda cc: yb_[:, cc * 128:(cc + 1) * 128], 4, ybr_, lambda b=b: yTAs[:, :, b * 128:(b + 1) * 128], r_yTAs, first_write=fw)
                                elif c in (2, 3, 7):
                                    gt = {2: gqB, 3: gkB, 7: gqD}[c]
                                    rsg = group_rstd(pb[:], pr, 8, 0, 4)
                                    t1, t1r = T()
                                    K.op("dve", lambda e: e.tensor_tensor(out=t1[:].rearrange("p (g d) -> p g d", g=8), in0=pb[:].rearrange("p (g d) -> p g d", g=8),
                                                                           in1=_bc(rsg.unsqueeze(2), [128, 8, 64]), op=ALU.mult), reads=[pr, smlr[4]], writes=[t1r])
                                    qb_, qbr_ = TB()
                                    if c == 7:
                                        K.op("dve", lambda e: e.tensor_tensor(out=t1[:], in0=t1[:], in1=gt[:], op=ALU.mult), reads=CR, writes=[t1r])
                                        rope(t1[:], t1r, 8, bi, qb_[:], qbr_)
                                    else:
                                        K.op("dve", lambda e: e.tensor_tensor(out=qb_[:], in0=t1[:], in1=gt[:], op=ALU.mult), reads=[t1r] + CR, writes=[qbr_])
                                    for _ in range(GAP):
                                        yield
                                    dstT, dstR = {2: (qTBs, r_qTBs), 3: (kTBs, r_kTBs), 7: (qTDs, r_qTDs)}[c]
                                    transpose_blocks(lambda cc: qb_[:, cc * 128:(cc + 1) * 128], 4, qbr_, lambda b=b, dstT=dstT: dstT[:, :, b * 128:(b + 1) * 128], dstR, first_write=fw,
                                                     ident=antib if c == 3 else None, evac="dve" if c == 3 else "act")
                                elif c == 4:
                                    vb_, vbr_ = TB()
                                    K.op("act", lambda e: e.activation(out=vb_[:], in_=pb[:], func=AF.Copy), reads=[pr], writes=[vbr_])
                                    for _ in range(GAP):
                                        yield
                                    p2, p2r = ps_next()
                                    K.op("pe", lambda e: e.matmul(p2[:], lhsT=antib[:], rhs=vb_[:], start=True, stop=True), reads=[vbr_] + GC, writes=[p2r])
                                    K.op("dve", lambda e, b=b: e.tensor_copy(out=vBs[:, b, :].rearrange("p (h d) -> p h d", h=8)[:, :, 0:64], in_=p2[:].rearrange("p (h d) -> p h d", h=8)),
                                         reads=[p2r], more=[r_vBs])
                                elif c == 8:
                                    rsg = group_rstd(pb[:, 0:128], pr, 2, 0, 5)
                                    t1, t1r = T()
                                    K.op("dve", lambda e: e.tensor_tensor(out=t1[:, 0:128].rearrange("p (g d) -> p g d", g=2), in0=pb[:, 0:128].rearrange("p (g d) -> p g d", g=2),
                                                                           in1=_bc(rsg.unsqueeze(2), [128, 2, 64]), op=ALU.mult), reads=[pr, smlr[5]], writes=[t1r])
                                    K.op("dve", lambda e: e.tensor_tensor(out=t1[:, 0:128], in0=t1[:, 0:128], in1=gkD[:], op=ALU.mult), reads=CR, writes=[t1r])
                                    kb_, kbr_ = TB()
                                    rope(t1[:, 0:128], t1r, 2, bi, kb_[:, 0:128], kbr_)
                                    K.op("dve", lambda e: e.tensor_copy(out=kb_[:, 128:384].rearrange("p (h r d) -> p h r d", h=2, r=2),
                                                                         in_=_bc(kb_[:, 0:128].rearrange("p (h d) -> p h d", h=2).unsqueeze(2), [128, 2, 2, 64])), reads=[kbr_], writes=[kbr_])
                                    K.op("act", lambda e, b=b: e.activation(out=vDs[:, b, :].rearrange("p (h d) -> p h d", h=2)[:, :, 0:64], in_=pb[:, 128:256].rearrange("p (h d) -> p h d", h=2), func=AF.Copy),
                                         reads=[pr], more=[r_vDs])
                                    for _ in range(GAP):
                                        yield
                                    transpose_blocks(lambda cc: kb_[:, 128 + cc * 128:128 + (cc + 1) * 128], 2, kbr_, lambda b=b: kTDs[:, :, b * 128:(b + 1) * 128], r_kTDs, first_write=fw)
                            live.append(post())
                            tick()
                        w_done()
                    while live:
                        tick()
                    tsl = slice(t * 512, (t + 1) * 512)

                    def store(dram, dres, sb, sres, view):
                        K.dma("sp", view, sb, reads=[sres], writes=[dres] if t == 0 else (), more=() if t == 0 else [dres], semres=dres)
                    store(qTB, R["qTB"][s], qTBs[:], r_qTBs, qTB[s].ap()[:, :, tsl].rearrange("c p n -> p c n"))
                    store(kTB, R["kTB"][s], kTBs[:], r_kTBs, kTB[s].ap()[:, :, tsl].rearrange("c p n -> p c n"))
                    store(vB, R["vB"][s], vBs[:], r_vBs, vB[s].ap()[t * 4:(t + 1) * 4].rearrange("b p n -> p b n"))
                    store(qTD, R["qTD"][s], qTDs[:], r_qTDs, qTD[s].ap()[:, :, tsl].rearrange("c p n -> p c n"))
                    store(kTD, R["kTD"][s], kTDs[:], r_kTDs, kTD[s].ap()[:, :, tsl].rearrange("c p n -> p c n"))
                    store(vD, R["vD"][s], vDs[:], r_vDs, vD[s].ap()[t * 4:(t + 1) * 4].rearrange("b p n -> p b n"))
                    store(yT, R["yT"][s], yTAs[:], r_yTAs, yT[s].ap()[0:4, :, tsl].rearrange("c p n -> p c n"))
                K.barrier()

        def conv_part(st, l, s):
            if True:
                cres = K.res("p1bconst")
                cwraw = SB(st, "cwraw", [32, 512]); vraw = SB(st, "vraw", [16, 128])
                cw = SB(st, "cw", [128, 4, 31]); cv = SB(st, "cv", [128, 16])
                K.dma("sp", cwraw[0:31, :], conv_w.ap()[l], writes=[cres], semres=cres)
                for i, srcv in enumerate((conv_b.ap()[l], conv_ln_g.ap()[l], conv_ln_b.ap()[l], mix_norm_g.ap()[l, 1024:1536])):
                    K.dma("sp", vraw[i * 4:(i + 1) * 4, :], srcv.rearrange("(c p) -> c p", p=128), more=[cres], semres=cres)
                c2 = K.res("p1bconst2")
                pb, pr = ps_next()
                for cc in range(4):
                    K.op("pe", lambda e, cc=cc: e.matmul(pb[:, cc * 32:cc * 32 + 31], lhsT=cwraw[0:31, cc * 128:(cc + 1) * 128], rhs=identf[0:31, 0:31], start=True, stop=True),
                         reads=[cres] + GC, writes=[pr] if cc == 0 else (), more=() if cc == 0 else [pr], inc=(cc == 3))
                K.op("act", lambda e: e.activation(out=cw[:], in_=pb[:, 0:128].rearrange("p (c n) -> p c n", c=4)[:, :, 0:31], func=AF.Copy), reads=[pr], writes=[c2])
                pb2, pr2 = ps_next()
                K.op("pe", lambda e: e.matmul(pb2[:, 0:16], lhsT=vraw[0:16, :], rhs=identf[0:16, 0:16], start=True, stop=True), reads=[cres] + GC, writes=[pr2])
                K.op("act", lambda e: e.activation(out=cv[:], in_=pb2[:, 0:16], func=AF.Copy), reads=[pr2], more=[c2])
                CR = [c2] + GC

                gl = [SB(st, f"gl{i}", [128, S + 30]) for i in range(2)]; glr = [K.res(f"gl{i}") for i in range(2)]
                acc = SB(st, "acc", [128, 4, S]); accr = [K.res(f"acc{i}") for i in range(4)]
                sq = SB(st, "sq1b", [128, 4, 512]); sqr = K.res("sq1b")
                yc = SB(st, "yc", [128, 4, 512]); ycr = K.res("yc")
                m_ = SB(st, "m1b", [128, 512]); m_r = K.res("m1b"); v_ = SB(st, "v1b", [128, 512]); v_r = K.res("v1b")
                xc = SB(st, "xc1b", [128, 512]); xcr = K.res("xc1b")
                yTs = [SB(st, f"yTCs{i}", [128, 4, 512], BF16) for i in range(2)]; yTsr = [K.res(f"yTCs{i}") for i in range(2)]
                for i in range(2):
                    K.op("dve", lambda e, i=i: e.memset(gl[i][:, 0:15], 0.0), writes=[glr[i]])
                    K.op("dve", lambda e, i=i: e.memset(gl[i][:, S + 15:S + 30], 0.0), more=[glr[i]])
                ops = []

                def mk_load(cc, g_, gr_):
                    return lambda: K.dma("sp", g_[:, 15:S + 15], gluT[s].ap()[cc], reads=[R["gluT"][s]], writes=[gr_], semres=gr_)

                def mk_tap(cc, g_, gr_, tap):
                    a_ = acc[:, cc, :]
                    if tap == 0:
                        return lambda: K.op("dve", lambda e: e.tensor_scalar(out=a_, in0=g_[:, 0:S], scalar1=cw[:, cc, 0:1], scalar2=cv[:, cc:cc + 1], op0=ALU.mult, op1=ALU.add),
                                            reads=[gr_] + CR, writes=[accr[cc]])
                    return lambda: K.op("dve", lambda e: e.scalar_tensor_tensor(out=a_, in0=g_[:, tap:tap + S], scalar=cw[:, cc, tap:tap + 1], in1=a_, op0=ALU.mult, op1=ALU.add),
                                        reads=[gr_] + CR, writes=[accr[cc]])
                for cc in range(4):
                    g_, gr_ = gl[cc % 2], glr[cc % 2]
                    ops.append(mk_load(cc, g_, gr_))
                    for tap in range(31):
                        ops.append(mk_tap(cc, g_, gr_, tap))

                def finish():
                    return [lambda tt=tt: finish_body(tt) for tt in range(4)]

                def finish_body(tt):
                    if True:
                        tsl = slice(tt * 512, (tt + 1) * 512)
                        pA, pAr = ps_next()
                        for cc in range(4):
                            K.op("pe", lambda e, cc=cc: e.matmul(pA[:], lhsT=onesf[:], rhs=acc[:, cc, tsl], start=(cc == 0), stop=(cc == 3)),
                                 reads=[accr[cc]] + GC, writes=[pAr] if cc == 0 else (), more=() if cc == 0 else [pAr], inc=(cc == 3))
                        K.op("act", lambda e: e.activation(out=sq[:], in_=acc[:, :, tsl], func=AF.Square), reads=accr, writes=[sqr])
                        pB, pBr = ps_next()
                        for cc in range(4):
                            K.op("pe", lambda e, cc=cc: e.matmul(pB[:], lhsT=onesf[:], rhs=sq[:, cc, :], start=(cc == 0), stop=(cc == 3)),
                                 reads=[sqr] + GC, writes=[pBr] if cc == 0 else (), more=() if cc == 0 else [pBr], inc=(cc == 3))
                        K.op("dve", lambda e: e.tensor_scalar(out=m_[:], in0=pA[:], scalar1=1.0 / 512, scalar2=None, op0=ALU.mult), reads=[pAr], writes=[m_r])
                        K.op("dve", lambda e: e.tensor_tensor(out=v_[:], in0=m_[:], in1=m_[:], op=ALU.mult), reads=[m_r], writes=[v_r])
                        K.op("dve", lambda e: e.scalar_tensor_tensor(out=v_[:], in0=pB[:], scalar=1.0 / 512, in1=v_[:], op0=ALU.mult, op1=ALU.subtract), reads=[pBr], writes=[v_r])
                        K.op("dve", lambda e: e.tensor_scalar(out=v_[:], in0=v_[:], scalar1=0.0, scalar2=epsr[:, 1:2], op0=ALU.max, op1=ALU.add), reads=GC, writes=[v_r])
                        K.op("act", lambda e: e.activation(out=v_[:], in_=v_[:], func=AF.Sqrt), reads=[v_r], writes=[v_r])
                        K.op("dve", lambda e: e.reciprocal(out=v_[:], in_=v_[:]), reads=[v_r], writes=[v_r])
                        for cc in range(4):
                            K.op("dve", lambda e, cc=cc: e.tensor_tensor(out=xc[:], in0=acc[:, cc, tsl], in1=m_[:], op=ALU.subtract), reads=[accr[cc], m_r], writes=[xcr])
                            K.op("dve", lambda e: e.tensor_tensor(out=xc[:], in0=xc[:], in1=v_[:], op=ALU.mult), reads=[v_r], writes=[xcr])
                            K.op("dve", lambda e, cc=cc: e.tensor_scalar(out=xc[:], in0=xc[:], scalar1=cv[:, 4 + cc:5 + cc], scalar2=cv[:, 8 + cc:9 + cc], op0=ALU.mult, op1=ALU.add),
                                 reads=CR, writes=[xcr])
                            K.op("act", lambda e, cc=cc: e.activation(out=yc[:, cc, :], in_=xc[:], func=AF.Silu), reads=[xcr], writes=[ycr] if cc == 0 else (), more=() if cc == 0 else [ycr])
                        K.op("act", lambda e: e.activation(out=sq[:], in_=yc[:], func=AF.Square), reads=[ycr], writes=[sqr])
                        pC, pCr = ps_next()
                        for cc in range(4):
                            K.op("pe", lambda e, cc=cc: e.matmul(pC[:], lhsT=onesf[:], rhs=sq[:, cc, :], start=(cc == 0), stop=(cc == 3)),
                                 reads=[sqr] + GC, writes=[pCr] if cc == 0 else (), more=() if cc == 0 else [pCr], inc=(cc == 3))
                        K.op("dve", lambda e: e.tensor_scalar(out=m_[:], in0=pC[:], scalar1=1.0 / 512, scalar2=epsr[:, 0:1], op0=ALU.mult, op1=ALU.add), reads=[pCr] + GC, writes=[m_r])
                        K.op("act", lambda e: e.activation(out=m_[:], in_=m_[:], func=AF.Sqrt), reads=[m_r], writes=[m_r])
                        K.op("dve", lambda e: e.reciprocal(out=m_[:], in_=m_[:]), reads=[m_r], writes=[m_r])
                        ys, ysr = yTs[tt % 2], yTsr[tt % 2]
                        for cc in range(4):
                            K.op("dve", lambda e, cc=cc: e.scalar_tensor_tensor(out=ys[:, cc, :], in0=yc[:, cc, :], scalar=cv[:, 12 + cc:13 + cc], in1=m_[:], op0=ALU.mult, op1=ALU.mult),
                                 reads=[ycr, m_r] + CR, writes=[ysr] if cc == 0 else (), more=() if cc == 0 else [ysr])
                        K.dma("sp", yT[s].ap()[8:12, :, tsl].rearrange("c p n -> p c n"), ys[:], reads=[ysr], more=[R["yT"][s]], semres=R["yT"][s])

                return ops, finish

        def phase1b(l, s):
            with ExitStack() as st:
                ops, fin = conv_part(st, l, s)
                for o in ops:
                    o()
                for f_ in fin():
                    f_()
                K.barrier()
                K.barrier()

        def phase2(l, s, kind):
            isB = kind == "B"
            ring["n"] = 6
            with ExitStack() as st:
                QT = SB(st, "QT", [128, 4, S], BF16); qr = K.res("QT")
                nkc = 4 if isB else 2
                KT = SB(st, "KT", [128, nkc, S], BF16); kr = K.res("KT")
                vw = 520 if isB else 130
                V = SB(st, "V", [128, 16, vw], BF16); vr = K.res("V")
                gain = SB(st, "gainBD", [128, 512]); gr = K.res("gainBD")
                qsrc, ksrc, vsrc = (qTB, kTB, vB) if isB else (qTD, kTD, vD)
                Rq, Rk, Rv = (R["qTB"][s], R["kTB"][s], R["vB"][s]) if isB else (R["qTD"][s], R["kTD"][s], R["vD"][s])
                for c in range(4):
                    K.dma("sp", QT[:, c, :], qsrc[s].ap()[c], reads=[Rq], writes=[qr] if c == 0 else (), more=() if c == 0 else [qr], semres=qr)
                for c in range(nkc):
                    K.dma("sp", KT[:, c, :], ksrc[s].ap()[c], reads=[Rk], writes=[kr] if c == 0 else (), more=() if c == 0 else [kr], semres=kr)
                for c in range(4):
                    K.dma("sp", V[:, c * 4:(c + 1) * 4, :], vsrc[s].ap()[c * 4:(c + 1) * 4].rearrange("b p n -> p b n"), reads=[Rv], writes=[vr] if c == 0 else (),
                          more=() if c == 0 else [vr], semres=vr)
                goff = 512 if isB else 1536
                K.dma("sp", gain[:], bass.AP(mix_norm_g, l * D + goff, [[0, 128], [1, 512]]), writes=[gr], semres=gr)
                MW = 2432
                if isB:
                    Mb = [SB(st, f"Mb{i}", [128, MW]) for i in range(2)]; Mr = [K.res(f"Mb{i}") for i in range(2)]
                LA = 5 if isB else 4
                NB_ = LA + 2
                if isB:
                    stt = [SB(st, f"stt{i}", [128, 512]) for i in range(NB_)]; sttr = [K.res(f"stt{i}") for i in range(NB_)]
                    conv_ops, conv_fin = [], []
                else:
                    conv_ops, conv_fin = conv_part(st, l, s)
                    conv_fin = conv_fin()
                PT = [SB(st, f"PT{i}", [128, 512], BF16) for i in range(NB_)]; PTr = [K.res(f"PT{i}") for i in range(NB_)]
                ytok = SB(st, "ytok", [128, 4, 512]); ytr = K.res("ytok")
                rec = SB(st, "rec", [128, 8]); recr = K.res("rec")
                sml = SB(st, "sml2", [128, 8]); smlr = K.res("sml2")
                junk = SB(st, "junk2", [128, 512], BF16); junkr = K.res("junk2")
                ybf = [SB(st, f"ybf{i}", [128, 512], BF16) for i in range(2)]; ybfr = [K.res(f"ybf{i}") for i in range(2)]
                yTs = [SB(st, f"yTs{i}", [128, 4, 512], BF16) for i in range(2)]; yTsr = [K.res(f"yTs{i}") for i in range(2)]
                items = []
                it = 0
                for j in range(4):
                    for h in range(8):
                        ilist = []
                        for i in range(16):
                            if isB:
                                Dij = i * 128 + 127 - j * 512
                                if Dij - 638 > 1024 or Dij < -1024:
                                    continue
                            ilist.append(i)
                        for ii, i in enumerate(ilist):
                            items.append({"j": j, "h": h, "ii": ii, "i": i, "n": len(ilist), "it": it})
                        it += 1

                def front(n, I_):
                    j, h, ii, i, it_ = I_["j"], I_["h"], I_["ii"], I_["i"], I_["it"]
                    hp = (h % 2) * 64
                    if ii == 0:
                        if conv_ops:
                            for _ in range(5):
                                if conv_ops:
                                    conv_ops.pop(0)()
                        elif conv_fin:
                            conv_fin.pop(0)()
                    if isB and ii == 0:
                        K.dma("sp", Mb[it_ % 2][:], bass.AP(biasR, h * 4096 + j * 512, [[1, 128], [1, MW]]), reads=[R["biasR"]], writes=[Mr[it_ % 2]], semres=Mr[it_ % 2])
                    pb, pr = ps_next()
                    kap = KT[hp:hp + 64, (h // 2) if isB else (h // 4), i * 128:(i + 1) * 128]
                    K.op("pe", lambda e: e.matmul(pb[:], lhsT=kap, rhs=QT[hp:hp + 64, h // 2, j * 512:(j + 1) * 512], start=True, stop=True),
                         reads=[kr, qr], writes=[pr])
                    P_, Pr_ = PT[n % NB_], PTr[n % NB_]
                    if isB:
                        s_, sr_ = stt[n % NB_], sttr[n % NB_]
                        M_, Mr_ = Mb[it_ % 2], Mr[it_ % 2]
                        K.op("dve", lambda e: e.tensor_tensor(out=s_[:], in0=pb[:], in1=M_[:, (15 - i) * 128:(15 - i) * 128 + 512], op=ALU.add),
                             reads=[pr, Mr_], writes=[sr_])
                        K.op("act", lambda e: e.activation(out=P_[:], in_=s_[:], func=AF.Exp), reads=[sr_], writes=[Pr_])
                    else:
                        K.op("act", lambda e: e.activation(out=P_[:], in_=pb[:], func=AF.Exp), reads=[pr], writes=[Pr_])

                def back(n, I_):
                    j, h, ii, i, it_, nI = I_["j"], I_["h"], I_["ii"], I_["i"], I_["it"], I_["n"]
                    P_, Pr_ = PT[n % NB_], PTr[n % NB_]
                    ob, obr = banks[6 + it_ % 2], bank_res[6 + it_ % 2]
                    ov = ob[:].rearrange("p (s n) -> p s n", s=4)
                    vcol = (h * 65) if isB else ((h // 4) * 65)
                    for sb_ in range(4):
                        K.op("pe", lambda e, sb_=sb_: e.matmul(ov[:, sb_, 0:65], lhsT=P_[:, sb_ * 128:(sb_ + 1) * 128], rhs=V[:, i, vcol:vcol + 65],
                                                               start=(ii == 0 and sb_ == 0), stop=(ii == nI - 1 and sb_ == 3)),
                             reads=[Pr_, vr], writes=[obr] if (ii == 0 and sb_ == 0) else (), more=() if (ii == 0 and sb_ == 0) else [obr], inc=(sb_ == 3))
                    if ii != nI - 1:
                        return
                    K.op("dve", lambda e: e.reciprocal(out=rec[:, 0:4], in_=ov[:, :, 64]), reads=[obr], writes=[recr])
                    K.op("dve", lambda e: e.tensor_tensor(out=ytok[:, :, h * 64:(h + 1) * 64], in0=ov[:, :, 0:64], in1=_bc(rec[:, 0:4].unsqueeze(2), [128, 4, 64]), op=ALU.mult),
                         reads=[obr, recr], writes=[ytr] if h == 0 else (), more=() if h == 0 else [ytr])
                    if h != 7:
                        return
                    ys, ysr = yTs[j % 2], yTsr[j % 2]
                    for sb_ in range(4):
                        K.op("act", lambda e, sb_=sb_: e.activation(out=junk[:], in_=ytok[:, sb_, :], func=AF.Square, accum_out=sml[:, 0:1]), reads=[ytr], writes=[junkr, smlr])
                        rstd_from_sum(st, sml[:, 0:1], 1, 1.0 / 512, 0, smlr, sml[:, 1:2], smlr, sml[:, 2:3], smlr)
                        yb_, ybr_ = ybf[sb_ % 2], ybfr[sb_ % 2]
                        K.op("dve", lambda e, sb_=sb_: e.scalar_tensor_tensor(out=yb_[:], in0=ytok[:, sb_, :], scalar=sml[:, 2:3], in1=gain[:], op0=ALU.mult, op1=ALU.mult),
                             reads=[ytr, smlr, gr], writes=[ybr_])
                        transpose_blocks(lambda cc: yb_[:, cc * 128:(cc + 1) * 128], 4, ybr_, lambda sb_=sb_: ys[:, :, sb_ * 128:(sb_ + 1) * 128], ysr, first_write=(sb_ == 0))
                    c0 = 4 if isB else 12
                    K.dma("sp", yT[s].ap()[c0:c0 + 4, :, j * 512:(j + 1) * 512].rearrange("c p n -> p c n"), ys[:], reads=[ysr], more=[R["yT"][s]], semres=R["yT"][s])

                for n in range(len(items) + LA):
                    if n < len(items):
                        front(n, items[n])
                    if n >= LA:
                        back(n - LA, items[n - LA])
                while conv_ops:
                    conv_ops.pop(0)()
                while conv_fin:
                    conv_fin.pop(0)()
                K.barrier()

        def phase3(l, s):
            src_x = x_in.ap()[s] if l == 0 else xres[s].ap()
            last = (l == DEPTH - 1)
            dst_x = out.ap()[s] if last else xres[s].ap()
            ring["n"] = 8
            with ExitStack() as st:
                cres = K.res("p3const")
                g2bc = SB(st, "g2bc", [128, D])
                K.dma("sp", g2bc[:], bass.AP(norm2_g, l * D, [[0, 128], [1, D]]), writes=[cres], semres=cres)
                yTt = SB(st, "yTt", [128, 16, 512], BF16); yTtr = K.res("yTt")
                xt = SB(st, "xt", [128, 4, D]); xtr = [K.res(f"xt{b}") for b in range(4)]
                junk = SB(st, "junk3", [128, D], BF16); junkr = K.res("junk3")
                hb = [SB(st, f"hb3{i}", [128, D], BF16) for i in range(2)]; hbr = [K.res(f"hb3{i}") for i in range(2)]
                h2T = SB(st, "h2T", [128, 16, 512], BF16); h2Tr = K.res("h2T")
                aT = SB(st, "aT", [128, 44, 512], BF16); aTr = K.res("aT")
                sgt = [SB(st, f"sgt{i}", [128, 512]) for i in range(2)]; sgtr = [K.res(f"sgt{i}") for i in range(2)]
                sml = SB(st, "sml3", [128, 8]); smlr = K.res("sml3")
                for t in range(4):
                    tsl = slice(t * 512, (t + 1) * 512)
                    K.dma("sp", yTt[:], yT[s].ap()[:, :, tsl].rearrange("c p n -> p c n"), reads=[R["yT"][s]], writes=[yTtr], semres=yTtr)
                    for b in range(4):
                        bi = t * 4 + b
                        K.dma("sp", xt[:, b, :], src_x[bi * 128:(bi + 1) * 128, :], reads=[R["xres"][s]] if l > 0 else (), writes=[xtr[b]], semres=xtr[b])
                    for n in range(4):
                        wsl, wr = w_get(("w_out", 0, 16, n * 512, 512))
                        for b in range(4):
                            pb, pr = ps_next()
                            for k in range(16):
                                K.op("pe", lambda e, k=k, b=b: e.matmul(pb[:], lhsT=yTt[:, k, b * 128:(b + 1) * 128], rhs=wsl[:, k, :], start=(k == 0), stop=(k == 15)),
                                     reads=[wr, yTtr], writes=[pr] if k == 0 else (), more=() if k == 0 else [pr], inc=(k == 15))
                            xa = xt[:, b, n * 512:(n + 1) * 512]
                            K.op("dve", lambda e, xa=xa: e.tensor_tensor(out=xa, in0=pb[:], in1=xa, op=ALU.add), reads=[pr], writes=[xtr[b]])
                        w_done()
                    def stageA3(b):
                        K.op("act", lambda e: e.activation(out=junk[:], in_=xt[:, b, :], func=AF.Square, accum_out=sml[:, 0:1]), reads=[xtr[b]], writes=[junkr, smlr])
                        rstd_from_sum(st, sml[:, 0:1], 1, 1.0 / D, 0, smlr, sml[:, 1:2], smlr, sml[:, 2:3], smlr)
                        K.op("dve", lambda e: e.scalar_tensor_tensor(out=hb[b % 2][:], in0=xt[:, b, :], scalar=sml[:, 2:3], in1=g2bc[:], op0=ALU.mult, op1=ALU.mult),
                             reads=[xtr[b], smlr, cres], writes=[hbr[b % 2]])

                    def stageB3(b):
                        for k0 in range(0, 16, 4):
                            transpose_blocks(lambda c, k0=k0: hb[b % 2][:, (k0 + c) * 128:(k0 + c + 1) * 128], 4, hbr[b % 2],
                                             lambda k0=k0, b=b: h2T[:, k0:k0 + 4, b * 128:(b + 1) * 128], h2Tr, first_write=(b == 0 and k0 == 0),
                                             evac="act" if (k0 // 4) % 2 == 0 else "dve")
                    stageA3(0)
                    for b in range(4):
                        if b + 1 < 4:
                            stageA3(b + 1)
                        stageB3(b)
                    for mg in range(11):
                        wg, wgr = w_get(("w_gate", 0, 16, mg * 512, 512))
                        wu, wur = w_get(("w_up", 0, 16, mg * 512, 512))
                        for mm in range(4):
                            m = mg * 4 + mm
                            pg, pgr = ps_next()
                            for k in range(16):
                                K.op("pe", lambda e, k=k, mm=mm: e.matmul(pg[:], lhsT=wg[:, k, mm * 128:(mm + 1) * 128], rhs=h2T[:, k, :], start=(k == 0), stop=(k == 15)),
                                     reads=[wgr, h2Tr], writes=[pgr] if k == 0 else (), more=() if k == 0 else [pgr], inc=(k == 15))
                            pu, pur = ps_next()
                            for k in range(16):
                                K.op("pe", lambda e, k=k, mm=mm: e.matmul(pu[:], lhsT=wu[:, k, mm * 128:(mm + 1) * 128], rhs=h2T[:, k, :], start=(k == 0), stop=(k == 15)),
                                     reads=[wur, h2Tr], writes=[pur] if k == 0 else (), more=() if k == 0 else [pur], inc=(k == 15))
                            sg, sgr = sgt[m % 2], sgtr[m % 2]
                            K.op("act", lambda e: e.activation(out=sg[:], in_=pg[:], func=AF.Silu), reads=[pgr], writes=[sgr])
                            K.op("dve", lambda e, m=m: e.tensor_tensor(out=aT[:, m, :], in0=sg[:], in1=pu[:], op=ALU.mult), reads=[sgr, pur],
                                 writes=[aTr] if m == 0 else (), more=() if m == 0 else [aTr])
                        w_done(); w_done()
                    for n in range(4):
                        pbs = [ps_next() for _ in range(4)]
                        for (k0, nk) in ((0, 16), (16, 16), (32, 12)):
                            wd, wdr = w_get(("w_down", k0, nk, n * 512, 512))
                            for b in range(4):
                                pb, pr = pbs[b]
                                for kk in range(nk):
                                    k = k0 + kk
                                    K.op("pe", lambda e, k=k, kk=kk, b=b, pb=pb: e.matmul(pb[:], lhsT=aT[:, k, b * 128:(b + 1) * 128], rhs=wd[:, kk, :], start=(k == 0), stop=(k == 43)),
                                         reads=[wdr, aTr], writes=[pr] if k == 0 else (), more=() if k == 0 else [pr], inc=(kk == nk - 1))
                            w_done()
                        for b in range(4):
                            pb, pr = pbs[b]
                            xa = xt[:, b, n * 512:(n + 1) * 512]
                            K.op("dve", lambda e, xa=xa, pb=pb: e.tensor_tensor(out=xa, in0=pb[:], in1=xa, op=ALU.add), reads=[pr], writes=[xtr[b]])
                    for b in range(4):
                        bi = t * 4 + b
                        dres = R["xres"][s] if not last else R.setdefault("out", K.res("out", local=False))
                        K.dma("sp", dst_x[bi * 128:(bi + 1) * 128, :], xt[:, b, :], reads=[xtr[b]], more=[dres], semres=dres)
                K.barrier()
            ring["n"] = 6

        stop = DBG["stop_after"]
        done = False
        for l in range(L):
            for s in range(NS):
                for nm, fn in (("p1", lambda: phase1(l, s)), ("p2B", lambda: phase2(l, s, "B")),
                               ("p2D", lambda: phase2(l, s, "D")), ("p3", lambda: phase3(l, s))):
                    if nm == "p3" and stop in ("p1", "p1b", "p2B", "p2D"):
                        continue
                    fn()
                    if stop == nm:
                        done = True
                        break
                if done:
                    break
            if done:
                break
        for key in list(K.semh.keys()):
            K.wait("sp", key, K.semtot[key])
        for f in ("pe", "act", "dve"):
            K.wait("sp", f, K.E[f].count)
    return nc


def _t5_bucket_np(rel):
    nb = 16
    max_exact = 8
    ret = np.where(rel > 0, nb, 0)
    n = np.abs(rel)
    nf = np.maximum(n, 1).astype(np.float32)
    large = max_exact + (np.log(nf / np.float32(max_exact)) / np.float32(math.log(1024 / max_exact)) * np.float32(nb - max_exact)).astype(np.int32)
    large = np.minimum(large, nb - 1)
    return ret + np.where(n < max_exact, n, large)


def _host_consts(rel_bias):
    ident = np.eye(128, dtype=np.float32)
    anti = np.ascontiguousarray(ident[::-1])
    pos = np.arange(S)
    row = (pos // 64).astype(np.float32)
    col = (pos % 64).astype(np.float32)
    freqs = (np.float32(10000.0) ** (-np.arange(16, dtype=np.float32) / np.float32(16))).astype(np.float32)
    ar = row[:, None] * freqs[None]
    ac = col[:, None] * freqs[None]
    cos = np.concatenate([np.cos(ar), np.cos(ar), np.cos(ac), np.cos(ac)], axis=1).astype(np.float32)
    sin = np.concatenate([-np.sin(ar), np.sin(ar), -np.sin(ac), np.sin(ac)], axis=1).astype(np.float32)
    u = np.arange(4096)
    rel = 2047 - u
    mult = ((np.abs(rel) <= 64).astype(np.int32) + ((rel % 4 == 0) & (np.abs(rel) <= 256)).astype(np.int32)
            + ((rel % 16 == 0) & (np.abs(rel) <= 1024)).astype(np.int32))
    lm = np.where(mult > 0, np.log(np.maximum(mult, 1).astype(np.float32)), np.float32(-1e30)).astype(np.float32)
    lmult = np.ascontiguousarray(np.broadcast_to(lm[None], (8, 4096))).astype(np.float32)
    bk = _t5_bucket_np(np.clip(rel, -2047, 2047))
    rbg = np.ascontiguousarray(np.asarray(rel_bias, dtype=np.float32)[bk, :].T)
    return {"c_ident": ident, "c_anti": anti, "c_cos": cos, "c_sin": sin, "c_rbg": rbg, "c_lmult": lmult}


def kernel(x, rel_bias, norm1_g, w_in, sgu_w, sgu_b, dil_qn_g, dil_kn_g, conv_w, conv_b, conv_ln_g, conv_ln_b,
           gqa_qn_g, gqa_kn_g, mix_norm_g, w_out, norm2_g, w_gate, w_up, w_down):
    ncores = DBG["ncores"]
    x = np.ascontiguousarray(np.asarray(x, dtype=np.float32))
    shared = {"norm1_g": norm1_g, "w_in": w_in, "sgu_w": sgu_w, "sgu_b": sgu_b, "dil_qn_g": dil_qn_g, "dil_kn_g": dil_kn_g,
              "conv_w": conv_w, "conv_b": conv_b, "conv_ln_g": conv_ln_g, "conv_ln_b": conv_ln_b, "gqa_qn_g": gqa_qn_g,
              "gqa_kn_g": gqa_kn_g, "mix_norm_g": mix_norm_g, "w_out": w_out, "norm2_g": norm2_g, "w_gate": w_gate,
              "w_up": w_up, "w_down": w_down}
    shared = {k: np.ascontiguousarray(np.asarray(v, dtype=np.float32)) for k, v in shared.items()}
    shared.update(_host_consts(rel_bias))
    nc = build_program()
    in_maps = []
    for c in range(ncores):
        m = dict(shared)
        m["x"] = x[c * NSEQ:(c + 1) * NSEQ]
        in_maps.append(m)
    res = run_bass_kernel_spmd(nc, in_maps, core_ids=list(range(ncores)))
    if DBG["dump"]:
        return res
    return np.concatenate([r["out"] for r in res.results], axis=0)
```

```python
import math
from contextlib import ExitStack

import numpy as np
import concourse.bass as bass
import concourse.mybir as mybir
from concourse.bass_utils import run_bass_kernel_spmd

F32 = mybir.dt.float32
BF16 = mybir.dt.bfloat16
AF = mybir.ActivationFunctionType
ALU = mybir.AluOpType
AX = mybir.AxisListType

D = 2048
S = 2048
DEPTH = 2
NSEQ = 2
INW = 4352
FFN = 5632
HD = 64
RMS_EPS = 1e-6
LN_EPS = 1e-5
NSLOT = 4

DBG = {"layers": DEPTH, "nseq": NSEQ, "stop_after": None, "dump": False, "ncores": 8}


class Res:
    __slots__ = ("name", "w", "r", "sem", "dcount", "key")

    def __init__(self, name):
        self.name = name
        self.w = {}
        self.r = {}
        self.sem = None
        self.dcount = 0
        self.key = "d:" + name


class Eng:
    def __init__(self, name, eng, sem):
        self.name = name
        self.eng = eng
        self.sem = sem
        self.count = 0
        self.waited = {}


class KB:
    def __init__(self, nc, es):
        self.nc = nc
        self.es = es
        self.E = {}
        for name, eng in (("pe", nc.tensor), ("act", nc.scalar), ("dve", nc.vector), ("pool", nc.gpsimd), ("sp", nc.sync)):
            sem = es.enter_context(nc.semaphore("sem_" + name)) if name in ("pe", "act", "dve", "pool") else None
            self.E[name] = Eng(name, eng, sem)
        self.dsem = {}
        self.local = []
        self.sem_pool = []
        self.semh = {}
        self.semtot = {}
        self.nres = 0

    def res(self, name, local=True):
        self.nres += 1
        r = Res(f"{name}_{self.nres}")
        if local:
            self.local.append(r)
        return r

    def _dma_sem(self, r):
        if r.sem is None:
            if self.sem_pool:
                r.sem, r.key, r.dcount = self.sem_pool.pop()
            else:
                r.sem = self.es.enter_context(self.nc.semaphore("ds_" + r.name))
            self.dsem[r.key] = r
            self.semh[r.key] = r.sem
            self.semtot[r.key] = r.dcount
        return r.sem

    def wait(self, eng, key, val):
        E = self.E[eng]
        if key == eng and eng == "pe":
            return
        if key in self.dsem:
            val = self.semtot[key]
            sem = self.semh[key]
        else:
            sem = self.E[key].sem
        if E.waited.get(key, 0) >= val:
            return
        E.eng.wait_ge(sem, val)
        E.waited[key] = val

    def _deps(self, eng, reads, writes, more):
        for r in reads:
            for k, v in r.w.items():
                self.wait(eng, k, v)
        for w in writes:
            for k, v in w.w.items():
                self.wait(eng, k, v)
            for k, v in w.r.items():
                self.wait(eng, k, v)
        for w in more:
            for k, v in w.r.items():
                self.wait(eng, k, v)

    def _record(self, key, val, reads, writes, more):
        for r in reads:
            if r.r.get(key, 0) < val:
                r.r[key] = val
        for w in writes:
            w.w = {key: val}
            w.r = {}
        for w in more:
            w.w[key] = val

    def op(self, eng, fn, reads=(), writes=(), more=(), inc=True):
        E = self.E[eng]
        self._deps(eng, reads, writes, more)
        ins = fn(E.eng)
        val = E.count + 1
        if inc:
            ins.then_inc(E.sem, 1)
            E.count = val
        self._record(eng, val, reads, writes, more)
        return ins

    def dma(self, q, out, in_, reads=(), writes=(), more=(), semres=None, **kw):
        E = self.E[q]
        self._deps(q, reads, writes, more)
        sem = self._dma_sem(semres)
        ins = E.eng.dma_start(out=out, in_=in_, **kw)
        ins.then_inc(sem, 16)
        semres.dcount += 16
        self.semtot[semres.key] = semres.dcount
        self._record(semres.key, semres.dcount, reads, writes, more)

    def barrier(self):
        evs = {}
        for r in self.local:
            for d in (r.w, r.r):
                for k, v in d.items():
                    if k in self.dsem:
                        evs[k] = max(evs.get(k, 0), v)
        for x in ("pe", "act", "dve", "pool", "sp"):
            for f in ("pe", "act", "dve", "pool"):
                if self.E[f].count:
                    self.wait(x, f, self.E[f].count)
            for k, v in evs.items():
                self.wait(x, k, v)
        for r in self.local:
            if r.sem is not None:
                self.sem_pool.append((r.sem, r.key, r.dcount))
                r.sem = None
        self.local = []


def _bc(ap, shape):
    return ap.broadcast_to(shape)


def build_program():
    nc = bass.Bass("TRN2", target_bir_lowering=False)
    L = DBG["layers"]
    NS = DBG["nseq"]
    dump = DBG["dump"]
    skind = "ExternalOutput" if dump else "Internal"

    def din(name, shape, dt=F32):
        return nc.dram_tensor(name, list(shape), dt, kind="ExternalInput")

    x_in = din("x", [NSEQ, S, D])
    norm1_g = din("norm1_g", [DEPTH, D]); w_in = din("w_in", [DEPTH, D, INW])
    sgu_w = din("sgu_w", [DEPTH, 8, 128, 128]); sgu_b = din("sgu_b", [DEPTH, 8, 128])
    dil_qn_g = din("dil_qn_g", [DEPTH, HD]); dil_kn_g = din("dil_kn_g", [DEPTH, HD])
    conv_w = din("conv_w", [DEPTH, 31, 512]); conv_b = din("conv_b", [DEPTH, 512])
    conv_ln_g = din("conv_ln_g", [DEPTH, 512]); conv_ln_b = din("conv_ln_b", [DEPTH, 512])
    gqa_qn_g = din("gqa_qn_g", [DEPTH, HD]); gqa_kn_g = din("gqa_kn_g", [DEPTH, HD])
    mix_norm_g = din("mix_norm_g", [DEPTH, D]); w_out = din("w_out", [DEPTH, D, D])
    norm2_g = din("norm2_g", [DEPTH, D])
    w_gate = din("w_gate", [DEPTH, D, FFN]); w_up = din("w_up", [DEPTH, D, FFN]); w_down = din("w_down", [DEPTH, FFN, D])
    c_ident = din("c_ident", [128, 128]); c_anti = din("c_anti", [128, 128])
    c_cos = din("c_cos", [S, HD]); c_sin = din("c_sin", [S, HD])
    c_rbg = din("c_rbg", [8, 4096]); c_lmult = din("c_lmult", [8, 4096])

    out = nc.dram_tensor("out", [NSEQ, S, D], F32, kind="ExternalOutput")

    def scr(name, shape, dt):
        return [nc.dram_tensor(f"{name}{s}", list(shape), dt, kind=skind) for s in range(NSEQ)]

    xres = scr("xres", [S, D], F32)
    qTB = scr("qTB", [4, 128, S], BF16); kTB = scr("kTB", [4, 128, S], BF16); vB = scr("vB", [16, 128, 520], BF16)
    qTD = scr("qTD", [4, 128, S], BF16); kTD = scr("kTD", [2, 128, S], BF16); vD = scr("vD", [16, 128, 130], BF16)
    gluT = scr("gluT", [4, 128, S], F32)
    yT = scr("yT", [16, 128, S], BF16)
    biasR = nc.dram_tensor("biasR", [8, 4096], F32, kind=skind)

    with ExitStack() as es:
        K = KB(nc, es)

        sbn = [0]

        def SB(st, name, shape, dt=F32):
            sbn[0] += 1
            return st.enter_context(nc.sbuf_tensor(f"{name}_{sbn[0]}", list(shape), dt))

        banks = [es.enter_context(nc.psum_tensor(f"bank{i}", [128, 512], F32)) for i in range(8)]
        bank_res = [K.res(f"bank{i}", local=False) for i in range(8)]
        ring = {"i": 0, "n": 6}

        def ps_next():
            i = ring["i"] % ring["n"]
            ring["i"] += 1
            return banks[i], bank_res[i]

        identb = SB(es, "identb", [128, 128], BF16); antib = SB(es, "antib", [128, 128], BF16)
        identf = SB(es, "identf", [128, 128], F32); onesf = SB(es, "onesf", [128, 128], F32)
        epsr = SB(es, "epsr", [128, 2], F32)
        gconst = K.res("gconst", local=False)
        K.dma("pool", identb[:], c_ident.ap(), writes=[gconst], semres=gconst)
        K.dma("pool", antib[:], c_anti.ap(), more=[gconst], semres=gconst)
        gconst3 = K.res("gconst3", local=False)
        K.dma("sp", identf[:], c_ident.ap(), writes=[gconst3], semres=gconst3)
        gconst2 = K.res("gconst2", local=False)
        K.op("dve", lambda e: e.memset(onesf[:], 1.0), writes=[gconst2])
        K.op("dve", lambda e: e.memset(epsr[:, 0:1], RMS_EPS), more=[gconst2])
        K.op("dve", lambda e: e.memset(epsr[:, 1:2], LN_EPS), more=[gconst2])
        GC = [gconst, gconst2, gconst3]

        wslots = [SB(es, f"wslot{i}", [128, 16, 512], BF16) for i in range(NSLOT)]
        wres = [K.res(f"wslot{i}", local=False) for i in range(NSLOT)]

        R = {}
        for nm in ("xres", "qTB", "kTB", "vB", "qTD", "kTD", "vD", "gluT", "yT"):
            R[nm] = [K.res(f"{nm}{s}", local=False) for s in range(NSEQ)]
        R["biasR"] = K.res("biasR", local=False)

        sched = []
        for l in range(L):
            for s in range(NS):
                for t in range(4):
                    for c in range(8):
                        sched.append((w_in, l, 0, 16, c * 512, 512))
                    sched.append((w_in, l, 0, 16, 4096, 256))
                for t in range(4):
                    for n in range(4):
                        sched.append((w_out, l, 0, 16, n * 512, 512))
                    for mg in range(11):
                        sched.append((w_gate, l, 0, 16, mg * 512, 512))
                        sched.append((w_up, l, 0, 16, mg * 512, 512))
                    for n in range(4):
                        for (k0, nk) in ((0, 16), (16, 16), (32, 12)):
                            sched.append((w_down, l, k0, nk, n * 512, 512))
        ws = {"issued": 0, "next": 0}

        def w_issue():
            i = ws["issued"]
            if i >= len(sched):
                return
            ws["issued"] += 1
            ten, l, k0, nk, c0, ncols = sched[i]
            slot = i % NSLOT
            src = ten.ap()[l, k0 * 128:(k0 + nk) * 128, c0:c0 + ncols].rearrange("(k p) n -> p k n", p=128)
            first = True
            for q0 in range(0, nk, 4):
                q1 = min(nk, q0 + 4)
                if first:
                    K.dma("pool", wslots[slot][:, q0:q1, 0:ncols], src[:, q0:q1, :], writes=[wres[slot]], semres=wres[slot])
                    first = False
                else:
                    K.dma("pool", wslots[slot][:, q0:q1, 0:ncols], src[:, q0:q1, :], more=[wres[slot]], semres=wres[slot])

        def w_get(expect=None):
            i = ws["next"]
            ws["next"] += 1
            assert i < ws["issued"]
            if expect is not None:
                assert sched[i][0] is expect[0] and sched[i][2:] == expect[1:], (sched[i], expect)
            return wslots[i % NSLOT], wres[i % NSLOT]

        def w_done():
            w_issue()

        for _ in range(NSLOT):
            w_issue()

        dbg_n = [0]

        def dbgdump(name, ap, res, shape, dt=F32):
            if not dump:
                return
            t_ = nc.dram_tensor("dbg_" + name, list(shape), dt, kind="ExternalOutput")
            r_ = K.res("dbg_" + name, local=False)
            K.dma("sp", t_.ap(), ap, reads=[res] if not isinstance(res, list) else res, writes=[r_], semres=r_)

        def rstd_from_sum(st, ss_ap, n, inv, eps_col, res_in, tmp, tmp_res, out_ap, out_res):
            K.op("dve", lambda e: e.tensor_scalar(out=tmp, in0=ss_ap, scalar1=inv, scalar2=epsr[:, eps_col:eps_col + 1],
                                                  op0=ALU.mult, op1=ALU.add), reads=[res_in] + GC, writes=[tmp_res])
            K.op("act", lambda e: e.activation(out=tmp, in_=tmp, func=AF.Sqrt), reads=[tmp_res], writes=[tmp_res])
            K.op("dve", lambda e: e.reciprocal(out=out_ap, in_=tmp), reads=[tmp_res], writes=[out_res])

        def transpose_blocks(src_ap_fn, nblk, src_res, dst_ap_fn, dst_res, first_write, ident=None, evac="act"):
            idm = identb if ident is None else ident
            pb, pr = ps_next()
            for c in range(nblk):
                K.op("pe", lambda e, c=c: e.matmul(pb[:, c * 128:(c + 1) * 128], lhsT=src_ap_fn(c), rhs=idm[:], start=True, stop=True),
                     reads=[src_res] + GC, writes=[pr] if c == 0 else (), more=() if c == 0 else [pr], inc=(c == nblk - 1))
            dst = dst_ap_fn()
            srcv = pb[:, 0:nblk * 128].rearrange("p (c n) -> p c n", c=nblk)
            if evac == "act":
                K.op("act", lambda e: e.activation(out=dst, in_=srcv, func=AF.Copy), reads=[pr],
                     writes=[dst_res] if first_write else (), more=() if first_write else [dst_res])
            else:
                K.op("dve", lambda e: e.tensor_copy(out=dst, in_=srcv), reads=[pr],
                     writes=[dst_res] if first_write else (), more=() if first_write else [dst_res])

        with ExitStack() as st:
            ta = SB(st, "bt_a", [8, 4096]); tb = SB(st, "bt_b", [8, 4096])
            ra = K.res("bt_a"); rb = K.res("bt_b")
            K.dma("sp", ta[:], c_rbg.ap(), writes=[ra], semres=ra)
            K.dma("sp", tb[:], c_lmult.ap(), writes=[rb], semres=rb)
            K.op("dve", lambda e: e.tensor_tensor(out=ta[:], in0=ta[:], in1=tb[:], op=ALU.add), reads=[rb], writes=[ra])
            K.dma("sp", biasR.ap(), ta[:], reads=[ra], writes=[R["biasR"]], semres=R["biasR"])
            K.barrier()

        def phase1(l, s):
            src_x = x_in.ap()[s] if l == 0 else xres[s].ap()
            with ExitStack() as st:
                cres = K.res("p1const")
                g1bc = SB(st, "g1bc", [128, D]); gqB = SB(st, "gqB", [128, 512]); gkB = SB(st, "gkB", [128, 512])
                gqD = SB(st, "gqD", [128, 512]); gkD = SB(st, "gkD", [128, 128]); gainA = SB(st, "gainA", [128, 512])
                cosT = SB(st, "cosT", [128, 16, HD]); sinT = SB(st, "sinT", [128, 16, HD])
                wsraw = SB(st, "wsraw", [128, 8, 128]); WsT = SB(st, "WsT", [128, 8, 128], BF16); bsT = SB(st, "bsT", [128, 8])
                first = [True]

                def cdma(dst, src, **kw):
                    if first[0]:
                        K.dma("sp", dst, src, writes=[cres], semres=cres, **kw); first[0] = False
                    else:
                        K.dma("sp", dst, src, more=[cres], semres=cres, **kw)
                cdma(g1bc[:], bass.AP(norm1_g, l * D, [[0, 128], [1, D]]))
                cdma(gqB[:].rearrange("p (h d) -> p h d", h=8), bass.AP(dil_qn_g, l * HD, [[0, 128], [0, 8], [1, HD]]))
                cdma(gkB[:].rearrange("p (h d) -> p h d", h=8), bass.AP(dil_kn_g, l * HD, [[0, 128], [0, 8], [1, HD]]))
                cdma(gqD[:].rearrange("p (h d) -> p h d", h=8), bass.AP(gqa_qn_g, l * HD, [[0, 128], [0, 8], [1, HD]]))
                cdma(gkD[:].rearrange("p (h d) -> p h d", h=2), bass.AP(gqa_kn_g, l * HD, [[0, 128], [0, 2], [1, HD]]))
                cdma(gainA[:], bass.AP(mix_norm_g, l * D, [[0, 128], [1, 512]]))
                cdma(cosT[:], c_cos.ap().rearrange("(b p) d -> p b d", p=128))
                cdma(sinT[:], c_sin.ap().rearrange("(b p) d -> p b d", p=128))
                cdma(wsraw[:], sgu_w.ap()[l].rearrange("g p q -> p g q"))
                cdma(bsT[:], sgu_b.ap()[l].rearrange("g p -> p g"), allow_slow_non_contiguous=True)
                c2 = K.res("p1const2")
                K.op("dve", lambda e: e.tensor_scalar(out=gqB[:], in0=gqB[:], scalar1=0.125, scalar2=None, op0=ALU.mult), reads=[cres], writes=[c2])
                K.op("dve", lambda e: e.tensor_scalar(out=gqD[:], in0=gqD[:], scalar1=0.125, scalar2=None, op0=ALU.mult), more=[c2])
                for g0 in (0, 4):
                    pb, pr = ps_next()
                    for g in range(g0, g0 + 4):
                        K.op("pe", lambda e, g=g: e.matmul(pb[:, (g - g0) * 128:(g - g0 + 1) * 128], lhsT=wsraw[:, g, :], rhs=identf[:], start=True, stop=True),
                             reads=[cres] + GC, writes=[pr] if g == g0 else (), more=() if g == g0 else [pr], inc=(g == g0 + 3))
                    K.op("act", lambda e: e.activation(out=WsT[:, g0:g0 + 4, :], in_=pb[:].rearrange("p (g n) -> p g n", g=4), func=AF.Copy),
                         reads=[pr], more=[c2])
                CR = [cres, c2] + GC

                xb = [SB(st, f"xb{i}", [128, D]) for i in range(2)]; xbr = [K.res(f"xb{i}") for i in range(2)]
                junk = SB(st, "junk", [128, D], BF16); junkr = K.res("junk")
                hb = [SB(st, f"hb{i}", [128, D], BF16) for i in range(2)]; hbr = [K.res(f"hb{i}") for i in range(2)]
                hT = SB(st, "hT", [128, 16, 512], BF16); hTr = K.res("hT")
                sml = SB(st, "sml", [128, 64]); smlr = [K.res(f"sml{i}") for i in range(8)]
                qTBs = SB(st, "qTBs", [128, 4, 512], BF16); kTBs = SB(st, "kTBs", [128, 4, 512], BF16)
                vBs = SB(st, "vBs", [128, 4, 520], BF16); qTDs = SB(st, "qTDs", [128, 4, 512], BF16)
                kTDs = SB(st, "kTDs", [128, 2, 512], BF16); vDs = SB(st, "vDs", [128, 4, 130], BF16)
                yTAs = SB(st, "yTAs", [128, 4, 512], BF16); glus = SB(st, "glus", [128, 4, 512]); ca = SB(st, "ca", [128, 4, 512])
                r_qTBs, r_kTBs, r_vBs, r_qTDs, r_kTDs, r_vDs, r_yTAs, r_glus, r_ca = [K.res(n) for n in
                    ("qTBs", "kTBs", "vBs", "qTDs", "kTDs", "vDs", "yTAs", "glus", "ca")]
                ug = SB(st, "ug", [128, 4, 512]); ugr = [K.res(f"ug{i}") for i in range(4)]
                NT = 5
                tmp = [SB(st, f"tmp{i}", [128, 512]) for i in range(NT)]; tmpr = [K.res(f"tmp{i}") for i in range(NT)]
                NTB = 6
                tbf = [SB(st, f"tbf{i}", [128, 512], BF16) for i in range(NTB)]; tbfr = [K.res(f"tbf{i}") for i in range(NTB)]
                tcnt = {"t": 0, "b": 0}

                def T():
                    i = tcnt["t"] % NT; tcnt["t"] += 1
                    return tmp[i], tmpr[i]

                def TB():
                    i = tcnt["b"] % NTB; tcnt["b"] += 1
                    return tbf[i], tbfr[i]
                K.op("dve", lambda e: e.memset(vBs[:], 1.0), writes=[r_vBs])
                K.op("dve", lambda e: e.memset(vDs[:], 1.0), writes=[r_vDs])

                def group_rstd(src_ap, src_res, ng, eps_col, slot):
                    sq, sqr = T()
                    K.op("act", lambda e: e.activation(out=sq[:, 0:ng * 64], in_=src_ap, func=AF.Square), reads=[src_res], writes=[sqr])
                    o = slot * 8
                    K.op("dve", lambda e: e.tensor_reduce(out=sml[:, o:o + ng], in_=sq[:, 0:ng * 64].rearrange("p (g d) -> p g d", g=ng), axis=AX.X, op=ALU.add),
                         reads=[sqr], writes=[smlr[slot]])
                    rstd_from_sum(st, sml[:, o:o + ng], ng, 1.0 / 64, eps_col, smlr[slot], sml[:, o:o + ng], smlr[slot], sml[:, o:o + ng], smlr[slot])
                    return sml[:, o:o + ng]

                def row_rstd(src_ap, src_res, n, slot):
                    o = slot * 8
                    K.op("act", lambda e: e.activation(out=junk[:, 0:n], in_=src_ap, func=AF.Square, accum_out=sml[:, o:o + 1]),
                         reads=[src_res], writes=[junkr, smlr[slot]])
                    rstd_from_sum(st, sml[:, o:o + 1], 1, 1.0 / n, 0, smlr[slot], sml[:, o + 1:o + 2], smlr[slot], sml[:, o + 2:o + 3], smlr[slot])
                    return sml[:, o + 2:o + 3]

                def rope(src, src_res, nh, bi, dst, dst_res):
                    t1, t1r = T(); t2, t2r = T()
                    w = nh * 64
                    cosb = _bc(cosT[:, bi, :].unsqueeze(1), [128, nh, HD])
                    K.op("dve", lambda e: e.tensor_tensor(out=t1[:, 0:w].rearrange("p (h d) -> p h d", h=nh), in0=src.rearrange("p (h d) -> p h d", h=nh), in1=cosb, op=ALU.mult),
                         reads=[src_res] + CR, writes=[t1r])
                    sv = src.rearrange("p (h a f i) -> p h a f i", h=nh, a=2, f=2)
                    tv = t2[:, 0:w].rearrange("p (h a f i) -> p h a f i", h=nh, a=2, f=2)
                    sn = sinT[:, bi, :].rearrange("p (a f i) -> p a f i", a=2, f=2)
                    for f in (0, 1):
                        sb_ = _bc(sn[:, :, f, :].unsqueeze(1), [128, nh, 2, 16])
                        K.op("dve", lambda e, f=f, sb_=sb_: e.tensor_tensor(out=tv[:, :, :, f, :], in0=sv[:, :, :, 1 - f, :], in1=sb_, op=ALU.mult),
                             reads=[src_res] + CR, writes=[t2r] if f == 0 else (), more=() if f == 0 else [t2r])
                    K.op("dve", lambda e: e.tensor_tensor(out=dst, in0=t1[:, 0:w], in1=t2[:, 0:w], op=ALU.add), reads=[t1r, t2r], writes=[dst_res])

                live = []
                GAP = 3

                def tick():
                    for g_ in list(live):
                        try:
                            next(g_)
                        except StopIteration:
                            live.remove(g_)
                ring["n"] = 8
                for t in range(4):
                    def stageA(t, b):
                        bi = t * 4 + b
                        xs_, xr_ = xb[bi % 2], xbr[bi % 2]
                        K.dma("sp", xs_[:], src_x[bi * 128:(bi + 1) * 128, :], reads=[R["xres"][s]] if l > 0 else (), writes=[xr_], semres=xr_)
                        rs = row_rstd(xs_[:], xr_, D, 0)
                        K.op("dve", lambda e: e.scalar_tensor_tensor(out=hb[b % 2][:], in0=xs_[:], scalar=rs, in1=g1bc[:], op0=ALU.mult, op1=ALU.mult),
                             reads=[xr_, smlr[0]] + CR, writes=[hbr[b % 2]])

                    def stageB(b):
                        for k0 in range(0, 16, 4):
                            transpose_blocks(lambda c, k0=k0: hb[b % 2][:, (k0 + c) * 128:(k0 + c + 1) * 128], 4, hbr[b % 2],
                                             lambda k0=k0, b=b: hT[:, k0:k0 + 4, b * 128:(b + 1) * 128], hTr, first_write=(b == 0 and k0 == 0),
                                             evac="act" if (k0 // 4) % 2 == 0 else "dve")
                    if t == 0:
                        stageA(0, 0)
                        stageA(0, 1)
                    stageB(0)
                    stageA(t, 2)
                    stageB(1)
                    stageA(t, 3)
                    stageB(2)
                    stageB(3)
                    for c in range(9):
                        ncols = 512 if c < 8 else 256
                        wsl, wr = w_get((w_in, 0, 16, c * 512, ncols))
                        if c in (5, 6):
                            for cc in range(4):
                                pb, pr = ps_next()
                                for k in range(16):
                                    K.op("pe", lambda e, k=k, cc=cc: e.matmul(pb[:], lhsT=wsl[:, k, cc * 128:(cc + 1) * 128], rhs=hT[:, k, :], start=(k == 0), stop=(k == 15)),
                                         reads=[wr, hTr], writes=[pr] if k == 0 else (), more=() if k == 0 else [pr], inc=(k == 15))
                                if c == 5:
                                    K.op("act", lambda e, cc=cc: e.activation(out=ca[:, cc, :], in_=pb[:], func=AF.Copy), reads=[pr],
                                         writes=[r_ca] if cc == 0 else (), more=() if cc == 0 else [r_ca])
                                else:
                                    sg, sgr = T()
                                    K.op("act", lambda e: e.activation(out=sg[:], in_=pb[:], func=AF.Sigmoid), reads=[pr], writes=[sgr])
                                    K.op("dve", lambda e, cc=cc: e.tensor_tensor(out=glus[:, cc, :], in0=ca[:, cc, :], in1=sg[:], op=ALU.mult), reads=[sgr, r_ca],
                                         writes=[r_glus] if cc == 0 else (), more=() if cc == 0 else [r_glus])
                                tick()
                            if c == 6 and t + 1 < 4:
                                stageA(t + 1, 0)
                                stageA(t + 1, 1)
                            if c == 6:
                                K.dma("sp", gluT[s].ap()[:, :, t * 512:(t + 1) * 512].rearrange("c p n -> p c n"), glus[:], reads=[r_glus], writes=[R["gluT"][s]] if t == 0 else (),
                                      more=() if t == 0 else [R["gluT"][s]], semres=R["gluT"][s])
                            w_done()
                            continue
                        for b in range(4):
                            bi = t * 4 + b
                            pb, pr = ps_next()
                            for k in range(16):
                                K.op("pe", lambda e, k=k, b=b: e.matmul(pb[:, 0:ncols], lhsT=hT[:, k, b * 128:(b + 1) * 128], rhs=wsl[:, k, 0:ncols], start=(k == 0), stop=(k == 15)),
                                     reads=[wr, hTr], writes=[pr] if k == 0 else (), more=() if k == 0 else [pr], inc=(k == 15))
                            def post(c=c, b=b, bi=bi, pb=pb, pr=pr):
                                fw = (b == 0)
                                if c == 0:
                                    K.op("act", lambda e, b=b: e.activation(out=ug[:, b, :], in_=pb[:], func=AF.Gelu_apprx_tanh), reads=[pr], writes=[ugr[b]])
                                elif c == 1:
                                    vg, vgr = T()
                                    K.op("act", lambda e: e.activation(out=vg[:], in_=pb[:], func=AF.Gelu_apprx_tanh), reads=[pr], writes=[vgr])
                                    vg3 = vg[:].rearrange("p (g d) -> p g d", g=8)
                                    K.op("dve", lambda e: e.tensor_reduce(out=sml[:, 8:16], in_=vg3, axis=AX.X, op=ALU.add), reads=[vgr], writes=[smlr[1]])
                                    K.op("dve", lambda e: e.tensor_scalar(out=sml[:, 8:16], in0=sml[:, 8:16], scalar1=1.0 / 64, scalar2=None, op0=ALU.mult), reads=[smlr[1]], writes=[smlr[1]])
                                    xc, xcr = T()
                                    K.op("dve", lambda e: e.tensor_tensor(out=xc[:].rearrange("p (g d) -> p g d", g=8), in0=vg3, in1=_bc(sml[:, 8:16].unsqueeze(2), [128, 8, 64]), op=ALU.subtract),
                                         reads=[vgr, smlr[1]], writes=[xcr])
                                    rsg = group_rstd(xc[:], xcr, 8, 1, 2)
                                    vn, vnr = TB()
                                    K.op("dve", lambda e: e.tensor_tensor(out=vn[:].rearrange("p (g d) -> p g d", g=8), in0=xc[:].rearrange("p (g d) -> p g d", g=8),
                                                                           in1=_bc(rsg.unsqueeze(2), [128, 8, 64]), op=ALU.mult), reads=[xcr, smlr[2]], writes=[vnr])
                                    for _ in range(GAP):
                                        yield
                                    p2, p2r = ps_next()
                                    for g in range(8):
                                        K.op("pe", lambda e, g=g: e.matmul(p2[:, g * 64:(g + 1) * 64], lhsT=WsT[:, g, :], rhs=vn[:, g * 64:(g + 1) * 64], start=True, stop=True),
                                             reads=[vnr] + CR, writes=[p2r] if g == 0 else (), more=() if g == 0 else [p2r], inc=(g == 7))
                                    ya, yar = T()
                                    K.op("dve", lambda e: e.tensor_tensor(out=ya[:].rearrange("p (g d) -> p g d", g=8), in0=p2[:].rearrange("p (g d) -> p g d", g=8),
                                                                           in1=_bc(bsT[:].unsqueeze(2), [128, 8, 64]), op=ALU.add), reads=[p2r] + CR, writes=[yar])
                                    K.op("dve", lambda e, b=b: e.tensor_tensor(out=ya[:], in0=ya[:], in1=ug[:, b, :], op=ALU.mult), reads=[ugr[b]], writes=[yar])
                                    if t == 0 and b == 0:
                                        dbgdump("ug", ug[:, 0, :], ugr[0], [128, 512]); dbgdump("vg", vg[:], vgr, [128, 512]); dbgdump("xc", xc[:], xcr, [128, 512])
                                        dbgdump("vn", vn[:], vnr, [128, 512], BF16); dbgdump("ya", ya[:], yar, [128, 512])
                                        dbgdump("WsT", WsT[:], CR, [128, 8, 128], BF16); dbgdump("bsT", bsT[:], CR, [128, 8]); dbgdump("sml", sml[:], smlr[2], [128, 64])
                                    rs = row_rstd(ya[:], yar, 512, 3)
                                    yb_, ybr_ = TB()
                                    K.op("dve", lambda e: e.scalar_tensor_tensor(out=yb_[:], in0=ya[:], scalar=rs, in1=gainA[:], op0=ALU.mult, op1=ALU.mult),
                                         reads=[yar, smlr[3]] + CR, writes=[ybr_])
                                    for _ in range(GAP):
                                        yield
                                    transpose_blocks(lambda cc: yb_[:, cc * 128:(cc + 1) * 128], 4, ybr_, lambda b=b: yTAs[:, :, b * 128:(b + 1) * 128], r_yTAs, first_write=fw)
                                elif c in (2, 3, 7):
                                    gt = {2: gqB, 3: gkB, 7: gqD}[c]
                                    rsg = group_rstd(pb[:], pr, 8, 0, 4)
                                    t1, t1r = T()
                                    K.op("dve", lambda e: e.tensor_tensor(out=t1[:].rearrange("p (g d) -> p g d", g=8), in0=pb[:].rearrange("p (g d) -> p g d", g=8),
                                                                           in1=_bc(rsg.unsqueeze(2), [128, 8, 64]), op=ALU.mult), reads=[pr, smlr[4]], writes=[t1r])
                                    qb_, qbr_ = TB()
                                    if c == 7:
                                        K.op("dve", lambda e: e.tensor_tensor(out=t1[:], in0=t1[:], in1=gt[:], op=ALU.mult), reads=CR, writes=[t1r])
                                        rope(t1[:], t1r, 8, bi, qb_[:], qbr_)
                                    else:
                                        K.op("dve", lambda e: e.tensor_tensor(out=qb_[:], in0=t1[:], in1=gt[:], op=ALU.mult), reads=[t1r] + CR, writes=[qbr_])
                                    for _ in range(GAP):
                                        yield
                                    dstT, dstR = {2: (qTBs, r_qTBs), 3: (kTBs, r_kTBs), 7: (qTDs, r_qTDs)}[c]
                                    transpose_blocks(lambda cc: qb_[:, cc * 128:(cc + 1) * 128], 4, qbr_, lambda b=b, dstT=dstT: dstT[:, :, b * 128:(b + 1) * 128], dstR, first_write=fw,
                                                     ident=antib if c == 3 else None, evac="dve" if c == 3 else "act")
                                elif c == 4:
                                    vb_, vbr_ = TB()
                                    K.op("act", lambda e: e.activation(out=vb_[:], in_=pb[:], func=AF.Copy), reads=[pr], writes=[vbr_])
                                    for _ in range(GAP):
                                        yield
                                    p2, p2r = ps_next()
                                    K.op("pe", lambda e: e.matmul(p2[:], lhsT=antib[:], rhs=vb_[:], start=True, stop=True), reads=[vbr_] + GC, writes=[p2r])
                                    K.op("dve", lambda e, b=b: e.tensor_copy(out=vBs[:, b, :].rearrange("p (h d) -> p h d", h=8)[:, :, 0:64], in_=p2[:].rearrange("p (h d) -> p h d", h=8)),
                                         reads=[p2r], more=[r_vBs])
                                elif c == 8:
                                    rsg = group_rstd(pb[:, 0:128], pr, 2, 0, 5)
                                    t1, t1r = T()
                                    K.op("dve", lambda e: e.tensor_tensor(out=t1[:, 0:128].rearrange("p (g d) -> p g d", g=2), in0=pb[:, 0:128].rearrange("p (g d) -> p g d", g=2),
                                                                           in1=_bc(rsg.unsqueeze(2), [128, 2, 64]), op=ALU.mult), reads=[pr, smlr[5]], writes=[t1r])
                                    K.op("dve", lambda e: e.tensor_tensor(out=t1[:, 0:128], in0=t1[:, 0:128], in1=gkD[:], op=ALU.mult), reads=CR, writes=[t1r])
                                    kb_, kbr_ = TB()
                                    rope(t1[:, 0:128], t1r, 2, bi, kb_[:, 0:128], kbr_)
                                    K.op("dve", lambda e: e.tensor_copy(out=kb_[:, 128:384].rearrange("p (h r d) -> p h r d", h=2, r=2),
                                                                         in_=_bc(kb_[:, 0:128].rearrange("p (h d) -> p h d", h=2).unsqueeze(2), [128, 2, 2, 64])), reads=[kbr_], writes=[kbr_])
                                    K.op("act", lambda e, b=b: e.activation(out=vDs[:, b, :].rearrange("p (h d) -> p h d", h=2)[:, :, 0:64], in_=pb[:, 128:256].rearrange("p (h d) -> p h d", h=2), func=AF.Copy),
                                         reads=[pr], more=[r_vDs])
                                    for _ in range(GAP):
                                        yield
                                    transpose_blocks(lambda cc: kb_[:, 128 + cc * 128:128 + (cc + 1) * 128], 2, kbr_, lambda b=b: kTDs[:, :, b * 128:(b + 1) * 128], r_kTDs, first_write=fw)
                            live.append(post())
                            tick()
                        w_done()
                    while live:
                        tick()
                    tsl = slice(t * 512, (t + 1) * 512)

                    def store(dram, dres, sb, sres, view):
                        K.dma("sp", view, sb, reads=[sres], writes=[dres] if t == 0 else (), more=() if t == 0 else [dres], semres=dres)
                    store(qTB, R["qTB"][s], qTBs[:], r_qTBs, qTB[s].ap()[:, :, tsl].rearrange("c p n -> p c n"))
                    store(kTB, R["kTB"][s], kTBs[:], r_kTBs, kTB[s].ap()[:, :, tsl].rearrange("c p n -> p c n"))
                    store(vB, R["vB"][s], vBs[:], r_vBs, vB[s].ap()[t * 4:(t + 1) * 4].rearrange("b p n -> p b n"))
                    store(qTD, R["qTD"][s], qTDs[:], r_qTDs, qTD[s].ap()[:, :, tsl].rearrange("c p n -> p c n"))
                    store(kTD, R["kTD"][s], kTDs[:], r_kTDs, kTD[s].ap()[:, :, tsl].rearrange("c p n -> p c n"))
                    store(vD, R["vD"][s], vDs[:], r_vDs, vD[s].ap()[t * 4:(t + 1) * 4].rearrange("b p n -> p b n"))
                    store(yT, R["yT"][s], yTAs[:], r_yTAs, yT[s].ap()[0:4, :, tsl].rearrange("c p n -> p c n"))
                K.barrier()

        def conv_part(st, l, s):
            if True:
                cres = K.res("p1bconst")
                cwraw = SB(st, "cwraw", [32, 512]); vraw = SB(st, "vraw", [16, 128])
                cw = SB(st, "cw", [128, 4, 31]); cv = SB(st, "cv", [128, 16])
                K.dma("sp", cwraw[0:31, :], conv_w.ap()[l], writes=[cres], semres=cres)
                for i, srcv in enumerate((conv_b.ap()[l], conv_ln_g.ap()[l], conv_ln_b.ap()[l], mix_norm_g.ap()[l, 1024:1536])):
                    K.dma("sp", vraw[i * 4:(i + 1) * 4, :], srcv.rearrange("(c p) -> c p", p=128), more=[cres], semres=cres)
                c2 = K.res("p1bconst2")
                pb, pr = ps_next()
                for cc in range(4):
                    K.op("pe", lambda e, cc=cc: e.matmul(pb[:, cc * 32:cc * 32 + 31], lhsT=cwraw[0:31, cc * 128:(cc + 1) * 128], rhs=identf[0:31, 0:31], start=True, stop=True),
                         reads=[cres] + GC, writes=[pr] if cc == 0 else (), more=() if cc == 0 else [pr], inc=(cc == 3))
                K.op("act", lambda e: e.activation(out=cw[:], in_=pb[:, 0:128].rearrange("p (c n) -> p c n", c=4)[:, :, 0:31], func=AF.Copy), reads=[pr], writes=[c2])
                pb2, pr2 = ps_next()
                K.op("pe", lambda e: e.matmul(pb2[:, 0:16], lhsT=vraw[0:16, :], rhs=identf[0:16, 0:16], start=True, stop=True), reads=[cres] + GC, writes=[pr2])
                K.op("act", lambda e: e.activation(out=cv[:], in_=pb2[:, 0:16], func=AF.Copy), reads=[pr2], more=[c2])
                CR = [c2] + GC

                gl = [SB(st, f"gl{i}", [128, S + 30]) for i in range(2)]; glr = [K.res(f"gl{i}") for i in range(2)]
                acc = SB(st, "acc", [128, 4, S]); accr = [K.res(f"acc{i}") for i in range(4)]
                sq = SB(st, "sq1b", [128, 4, 512]); sqr = K.res("sq1b")
                yc = SB(st, "yc", [128, 4, 512]); ycr = K.res("yc")
                m_ = SB(st, "m1b", [128, 512]); m_r = K.res("m1b"); v_ = SB(st, "v1b", [128, 512]); v_r = K.res("v1b")
                xc = SB(st, "xc1b", [128, 512]); xcr = K.res("xc1b")
                yTs = [SB(st, f"yTCs{i}", [128, 4, 512], BF16) for i in range(2)]; yTsr = [K.res(f"yTCs{i}") for i in range(2)]
                for i in range(2):
                    K.op("dve", lambda e, i=i: e.memset(gl[i][:, 0:15], 0.0), writes=[glr[i]])
                    K.op("dve", lambda e, i=i: e.memset(gl[i][:, S + 15:S + 30], 0.0), more=[glr[i]])
                ops = []

                def mk_load(cc, g_, gr_):
                    return lambda: K.dma("sp", g_[:, 15:S + 15], gluT[s].ap()[cc], reads=[R["gluT"][s]], writes=[gr_], semres=gr_)

                def mk_tap(cc, g_, gr_, tap):
                    a_ = acc[:, cc, :]
                    ceng = CONV_ENG[cc]
                    if tap == 0:
                        return lambda: K.op(ceng, lambda e: e.tensor_scalar(out=a_, in0=g_[:, 0:S], scalar1=cw[:, cc, 0:1], scalar2=cv[:, cc:cc + 1], op0=ALU.mult, op1=ALU.add),
                                            reads=[gr_] + CR, writes=[accr[cc]])
                    return lambda: K.op(ceng, lambda e: e.scalar_tensor_tensor(out=a_, in0=g_[:, tap:tap + S], scalar=cw[:, cc, tap:tap + 1], in1=a_, op0=ALU.mult, op1=ALU.add),
                                        reads=[gr_] + CR, writes=[accr[cc]])
                CONV_ENG = ("dve", "dve", "dve", "dve")
                lanes = {}
                for cc in range(4):
                    bi_ = cc % 2
                    g_, gr_ = gl[bi_], glr[bi_]
                    ln_ = lanes.setdefault(CONV_ENG[cc], [])
                    ln_.append(mk_load(cc, g_, gr_))
                    for tap in range(31):
                        ln_.append(mk_tap(cc, g_, gr_, tap))
                lane_list = list(lanes.values())
                while any(lane_list):
                    for ln_ in lane_list:
                        if ln_:
                            ops.append(ln_.pop(0))

                def finish():
                    return [lambda tt=tt: finish_body(tt) for tt in range(4)]

                def finish_body(tt):
                    if True:
                        tsl = slice(tt * 512, (tt + 1) * 512)
                        pA, pAr = ps_next()
                        for cc in range(4):
                            K.op("pe", lambda e, cc=cc: e.matmul(pA[:], lhsT=onesf[:], rhs=acc[:, cc, tsl], start=(cc == 0), stop=(cc == 3)),
                                 reads=[accr[cc]] + GC, writes=[pAr] if cc == 0 else (), more=() if cc == 0 else [pAr], inc=(cc == 3))
                        K.op("act", lambda e: e.activation(out=sq[:], in_=acc[:, :, tsl], func=AF.Square), reads=accr, writes=[sqr])
                        pB, pBr = ps_next()
                        for cc in range(4):
                            K.op("pe", lambda e, cc=cc: e.matmul(pB[:], lhsT=onesf[:], rhs=sq[:, cc, :], start=(cc == 0), stop=(cc == 3)),
                                 reads=[sqr] + GC, writes=[pBr] if cc == 0 else (), more=() if cc == 0 else [pBr], inc=(cc == 3))
                        K.op("dve", lambda e: e.tensor_scalar(out=m_[:], in0=pA[:], scalar1=1.0 / 512, scalar2=None, op0=ALU.mult), reads=[pAr], writes=[m_r])
                        K.op("dve", lambda e: e.tensor_tensor(out=v_[:], in0=m_[:], in1=m_[:], op=ALU.mult), reads=[m_r], writes=[v_r])
                        K.op("dve", lambda e: e.scalar_tensor_tensor(out=v_[:], in0=pB[:], scalar=1.0 / 512, in1=v_[:], op0=ALU.mult, op1=ALU.subtract), reads=[pBr], writes=[v_r])
                        K.op("dve", lambda e: e.tensor_scalar(out=v_[:], in0=v_[:], scalar1=0.0, scalar2=epsr[:, 1:2], op0=ALU.max, op1=ALU.add), reads=GC, writes=[v_r])
                        K.op("act", lambda e: e.activation(out=v_[:], in_=v_[:], func=AF.Sqrt), reads=[v_r], writes=[v_r])
                        K.op("dve", lambda e: e.reciprocal(out=v_[:], in_=v_[:]), reads=[v_r], writes=[v_r])
                        for cc in range(4):
                            K.op("dve", lambda e, cc=cc: e.tensor_tensor(out=xc[:], in0=acc[:, cc, tsl], in1=m_[:], op=ALU.subtract), reads=[accr[cc], m_r], writes=[xcr])
                            K.op("dve", lambda e: e.tensor_tensor(out=xc[:], in0=xc[:], in1=v_[:], op=ALU.mult), reads=[v_r], writes=[xcr])
                            K.op("dve", lambda e, cc=cc: e.tensor_scalar(out=xc[:], in0=xc[:], scalar1=cv[:, 4 + cc:5 + cc], scalar2=cv[:, 8 + cc:9 + cc], op0=ALU.mult, op1=ALU.add),
                                 reads=CR, writes=[xcr])
                            K.op("act", lambda e, cc=cc: e.activation(out=yc[:, cc, :], in_=xc[:], func=AF.Silu), reads=[xcr], writes=[ycr] if cc == 0 else (), more=() if cc == 0 else [ycr])
                        K.op("act", lambda e: e.activation(out=sq[:], in_=yc[:], func=AF.Square), reads=[ycr], writes=[sqr])
                        pC, pCr = ps_next()
                        for cc in range(4):
                            K.op("pe", lambda e, cc=cc: e.matmul(pC[:], lhsT=onesf[:], rhs=sq[:, cc, :], start=(cc == 0), stop=(cc == 3)),
                                 reads=[sqr] + GC, writes=[pCr] if cc == 0 else (), more=() if cc == 0 else [pCr], inc=(cc == 3))
                        K.op("dve", lambda e: e.tensor_scalar(out=m_[:], in0=pC[:], scalar1=1.0 / 512, scalar2=epsr[:, 0:1], op0=ALU.mult, op1=ALU.add), reads=[pCr] + GC, writes=[m_r])
                        K.op("act", lambda e: e.activation(out=m_[:], in_=m_[:], func=AF.Sqrt), reads=[m_r], writes=[m_r])
                        K.op("dve", lambda e: e.reciprocal(out=m_[:], in_=m_[:]), reads=[m_r], writes=[m_r])
                        ys, ysr = yTs[tt % 2], yTsr[tt % 2]
                        for cc in range(4):
                            K.op("dve", lambda e, cc=cc: e.scalar_tensor_tensor(out=ys[:, cc, :], in0=yc[:, cc, :], scalar=cv[:, 12 + cc:13 + cc], in1=m_[:], op0=ALU.mult, op1=ALU.mult),
                                 reads=[ycr, m_r] + CR, writes=[ysr] if cc == 0 else (), more=() if cc == 0 else [ysr])
                        K.dma("sp", yT[s].ap()[8:12, :, tsl].rearrange("c p n -> p c n"), ys[:], reads=[ysr], more=[R["yT"][s]], semres=R["yT"][s])

                return ops, finish

        def phase1b(l, s):
            with ExitStack() as st:
                ops, fin = conv_part(st, l, s)
                for o in ops:
                    o()
                for f_ in fin():
                    f_()
                K.barrier()
                K.barrier()

        def phase2(l, s, kind):
            isB = kind == "B"
            ring["n"] = 6
            with ExitStack() as st:
                QT = SB(st, "QT", [128, 4, S], BF16); qr = K.res("QT")
                nkc = 4 if isB else 2
                KT = SB(st, "KT", [128, nkc, S], BF16); kr = K.res("KT")
                vw = 520 if isB else 130
                V = SB(st, "V", [128, 16, vw], BF16); vr = K.res("V")
                gain = SB(st, "gainBD", [128, 512]); gr = K.res("gainBD")
                qsrc, ksrc, vsrc = (qTB, kTB, vB) if isB else (qTD, kTD, vD)
                Rq, Rk, Rv = (R["qTB"][s], R["kTB"][s], R["vB"][s]) if isB else (R["qTD"][s], R["kTD"][s], R["vD"][s])
                for c in range(4):
                    K.dma("sp", QT[:, c, :], qsrc[s].ap()[c], reads=[Rq], writes=[qr] if c == 0 else (), more=() if c == 0 else [qr], semres=qr)
                for c in range(nkc):
                    K.dma("sp", KT[:, c, :], ksrc[s].ap()[c], reads=[Rk], writes=[kr] if c == 0 else (), more=() if c == 0 else [kr], semres=kr)
                for c in range(4):
                    K.dma("sp", V[:, c * 4:(c + 1) * 4, :], vsrc[s].ap()[c * 4:(c + 1) * 4].rearrange("b p n -> p b n"), reads=[Rv], writes=[vr] if c == 0 else (),
                          more=() if c == 0 else [vr], semres=vr)
                goff = 512 if isB else 1536
                K.dma("sp", gain[:], bass.AP(mix_norm_g, l * D + goff, [[0, 128], [1, 512]]), writes=[gr], semres=gr)
                MW = 2432
                if isB:
                    Mb = [SB(st, f"Mb{i}", [128, MW]) for i in range(2)]; Mr = [K.res(f"Mb{i}") for i in range(2)]
                LA = 5 if isB else 4
                NB_ = LA + 2
                if isB:
                    stt = [SB(st, f"stt{i}", [128, 512]) for i in range(NB_)]; sttr = [K.res(f"stt{i}") for i in range(NB_)]
                    conv_ops, conv_fin = [], []
                else:
                    conv_ops, conv_fin = conv_part(st, l, s)
                    conv_fin = conv_fin()
                PT = [SB(st, f"PT{i}", [128, 512], BF16) for i in range(NB_)]; PTr = [K.res(f"PT{i}") for i in range(NB_)]
                ytok = SB(st, "ytok", [128, 4, 512]); ytr = K.res("ytok")
                rec = SB(st, "rec", [128, 8]); recr = K.res("rec")
                sml = SB(st, "sml2", [128, 8]); smlr = K.res("sml2")
                junk = SB(st, "junk2", [128, 512], BF16); junkr = K.res("junk2")
                ybf = [SB(st, f"ybf{i}", [128, 512], BF16) for i in range(2)]; ybfr = [K.res(f"ybf{i}") for i in range(2)]
                yTs = [SB(st, f"yTs{i}", [128, 4, 512], BF16) for i in range(2)]; yTsr = [K.res(f"yTs{i}") for i in range(2)]
                items = []
                it = 0
                for j in range(4):
                    for h in range(8):
                        ilist = []
                        for i in range(16):
                            if isB:
                                Dij = i * 128 + 127 - j * 512
                                if Dij - 638 > 1024 or Dij < -1024:
                                    continue
                            ilist.append(i)
                        for ii, i in enumerate(ilist):
                            items.append({"j": j, "h": h, "ii": ii, "i": i, "n": len(ilist), "it": it})
                        it += 1

                def front(n, I_):
                    j, h, ii, i, it_ = I_["j"], I_["h"], I_["ii"], I_["i"], I_["it"]
                    hp = (h % 2) * 64
                    if ii == 0:
                        if conv_ops:
                            for _ in range(5):
                                if conv_ops:
                                    conv_ops.pop(0)()
                        elif conv_fin:
                            conv_fin.pop(0)()
                    if isB and ii == 0:
                        K.dma("sp", Mb[it_ % 2][:], bass.AP(biasR, h * 4096 + j * 512, [[1, 128], [1, MW]]), reads=[R["biasR"]], writes=[Mr[it_ % 2]], semres=Mr[it_ % 2])
                    pb, pr = ps_next()
                    kap = KT[hp:hp + 64, (h // 2) if isB else (h // 4), i * 128:(i + 1) * 128]
                    K.op("pe", lambda e: e.matmul(pb[:], lhsT=kap, rhs=QT[hp:hp + 64, h // 2, j * 512:(j + 1) * 512], start=True, stop=True),
                         reads=[kr, qr], writes=[pr])
                    P_, Pr_ = PT[n % NB_], PTr[n % NB_]
                    if isB:
                        s_, sr_ = stt[n % NB_], sttr[n % NB_]
                        M_, Mr_ = Mb[it_ % 2], Mr[it_ % 2]
                        K.op("dve", lambda e: e.tensor_tensor(out=s_[:], in0=pb[:], in1=M_[:, (15 - i) * 128:(15 - i) * 128 + 512], op=ALU.add),
                             reads=[pr, Mr_], writes=[sr_])
                        K.op("act", lambda e: e.activation(out=P_[:], in_=s_[:], func=AF.Exp), reads=[sr_], writes=[Pr_])
                    else:
                        K.op("act", lambda e: e.activation(out=P_[:], in_=pb[:], func=AF.Exp), reads=[pr], writes=[Pr_])

                def back(n, I_):
                    j, h, ii, i, it_, nI = I_["j"], I_["h"], I_["ii"], I_["i"], I_["it"], I_["n"]
                    P_, Pr_ = PT[n % NB_], PTr[n % NB_]
                    ob, obr = banks[6 + it_ % 2], bank_res[6 + it_ % 2]
                    ov = ob[:].rearrange("p (s n) -> p s n", s=4)
                    vcol = (h * 65) if isB else ((h // 4) * 65)
                    for sb_ in range(4):
                        K.op("pe", lambda e, sb_=sb_: e.matmul(ov[:, sb_, 0:65], lhsT=P_[:, sb_ * 128:(sb_ + 1) * 128], rhs=V[:, i, vcol:vcol + 65],
                                                               start=(ii == 0 and sb_ == 0), stop=(ii == nI - 1 and sb_ == 3)),
                             reads=[Pr_, vr], writes=[obr] if (ii == 0 and sb_ == 0) else (), more=() if (ii == 0 and sb_ == 0) else [obr], inc=(sb_ == 3))
                    if ii != nI - 1:
                        return
                    K.op("dve", lambda e: e.reciprocal(out=rec[:, 0:4], in_=ov[:, :, 64]), reads=[obr], writes=[recr])
                    K.op("dve", lambda e: e.tensor_tensor(out=ytok[:, :, h * 64:(h + 1) * 64], in0=ov[:, :, 0:64], in1=_bc(rec[:, 0:4].unsqueeze(2), [128, 4, 64]), op=ALU.mult),
                         reads=[obr, recr], writes=[ytr] if h == 0 else (), more=() if h == 0 else [ytr])
                    if h != 7:
                        return
                    ys, ysr = yTs[j % 2], yTsr[j % 2]
                    for sb_ in range(4):
                        K.op("act", lambda e, sb_=sb_: e.activation(out=junk[:], in_=ytok[:, sb_, :], func=AF.Square, accum_out=sml[:, 0:1]), reads=[ytr], writes=[junkr, smlr])
                        rstd_from_sum(st, sml[:, 0:1], 1, 1.0 / 512, 0, smlr, sml[:, 1:2], smlr, sml[:, 2:3], smlr)
                        yb_, ybr_ = ybf[sb_ % 2], ybfr[sb_ % 2]
                        K.op("dve", lambda e, sb_=sb_: e.scalar_tensor_tensor(out=yb_[:], in0=ytok[:, sb_, :], scalar=sml[:, 2:3], in1=gain[:], op0=ALU.mult, op1=ALU.mult),
                             reads=[ytr, smlr, gr], writes=[ybr_])
                        transpose_blocks(lambda cc: yb_[:, cc * 128:(cc + 1) * 128], 4, ybr_, lambda sb_=sb_: ys[:, :, sb_ * 128:(sb_ + 1) * 128], ysr, first_write=(sb_ == 0))
                    c0 = 4 if isB else 12
                    K.dma("sp", yT[s].ap()[c0:c0 + 4, :, j * 512:(j + 1) * 512].rearrange("c p n -> p c n"), ys[:], reads=[ysr], more=[R["yT"][s]], semres=R["yT"][s])

                for n in range(len(items) + LA):
                    if n < len(items):
                        front(n, items[n])
                    if n >= LA:
                        back(n - LA, items[n - LA])
                while conv_ops:
                    conv_ops.pop(0)()
                while conv_fin:
                    conv_fin.pop(0)()
                K.barrier()

        def phase3(l, s):
            src_x = x_in.ap()[s] if l == 0 else xres[s].ap()
            last = (l == DEPTH - 1)
            dst_x = out.ap()[s] if last else xres[s].ap()
            ring["n"] = 8
            with ExitStack() as st:
                cres = K.res("p3const")
                g2bc = SB(st, "g2bc", [128, D])
                K.dma("sp", g2bc[:], bass.AP(norm2_g, l * D, [[0, 128], [1, D]]), writes=[cres], semres=cres)
                yTt = SB(st, "yTt", [128, 16, 512], BF16); yTtr = K.res("yTt")
                xt = SB(st, "xt", [128, 4, D]); xtr = [K.res(f"xt{b}") for b in range(4)]
                junk = SB(st, "junk3", [128, D], BF16); junkr = K.res("junk3")
                hb = [SB(st, f"hb3{i}", [128, D], BF16) for i in range(2)]; hbr = [K.res(f"hb3{i}") for i in range(2)]
                h2T = SB(st, "h2T", [128, 16, 512], BF16); h2Tr = K.res("h2T")
                aT = SB(st, "aT", [128, 44, 512], BF16); aTr = K.res("aT")
                sgt = [SB(st, f"sgt{i}", [128, 512]) for i in range(2)]; sgtr = [K.res(f"sgt{i}") for i in range(2)]
                sml = SB(st, "sml3", [128, 8]); smlr = K.res("sml3")
                for t in range(4):
                    tsl = slice(t * 512, (t + 1) * 512)
                    K.dma("sp", yTt[:], yT[s].ap()[:, :, tsl].rearrange("c p n -> p c n"), reads=[R["yT"][s]], writes=[yTtr], semres=yTtr)
                    for b in range(4):
                        bi = t * 4 + b
                        K.dma("sp", xt[:, b, :], src_x[bi * 128:(bi + 1) * 128, :], reads=[R["xres"][s]] if l > 0 else (), writes=[xtr[b]], semres=xtr[b])
                    for n in range(4):
                        wsl, wr = w_get((w_out, 0, 16, n * 512, 512))
                        for b in range(4):
                            pb, pr = ps_next()
                            for k in range(16):
                                K.op("pe", lambda e, k=k, b=b: e.matmul(pb[:], lhsT=yTt[:, k, b * 128:(b + 1) * 128], rhs=wsl[:, k, :], start=(k == 0), stop=(k == 15)),
                                     reads=[wr, yTtr], writes=[pr] if k == 0 else (), more=() if k == 0 else [pr], inc=(k == 15))
                            xa = xt[:, b, n * 512:(n + 1) * 512]
                            K.op("dve", lambda e, xa=xa: e.tensor_tensor(out=xa, in0=pb[:], in1=xa, op=ALU.add), reads=[pr], writes=[xtr[b]])
                        w_done()
                    def stageA3(b):
                        K.op("act", lambda e: e.activation(out=junk[:], in_=xt[:, b, :], func=AF.Square, accum_out=sml[:, 0:1]), reads=[xtr[b]], writes=[junkr, smlr])
                        rstd_from_sum(st, sml[:, 0:1], 1, 1.0 / D, 0, smlr, sml[:, 1:2], smlr, sml[:, 2:3], smlr)
                        K.op("dve", lambda e: e.scalar_tensor_tensor(out=hb[b % 2][:], in0=xt[:, b, :], scalar=sml[:, 2:3], in1=g2bc[:], op0=ALU.mult, op1=ALU.mult),
                             reads=[xtr[b], smlr, cres], writes=[hbr[b % 2]])

                    def stageB3(b):
                        for k0 in range(0, 16, 4):
                            transpose_blocks(lambda c, k0=k0: hb[b % 2][:, (k0 + c) * 128:(k0 + c + 1) * 128], 4, hbr[b % 2],
                                             lambda k0=k0, b=b: h2T[:, k0:k0 + 4, b * 128:(b + 1) * 128], h2Tr, first_write=(b == 0 and k0 == 0),
                                             evac="act" if (k0 // 4) % 2 == 0 else "dve")
                    stageA3(0)
                    for b in range(4):
                        if b + 1 < 4:
                            stageA3(b + 1)
                        stageB3(b)
                    for mg in range(11):
                        wg, wgr = w_get((w_gate, 0, 16, mg * 512, 512))
                        wu, wur = w_get((w_up, 0, 16, mg * 512, 512))
                        for mm in range(4):
                            m = mg * 4 + mm
                            pg, pgr = ps_next()
                            for k in range(16):
                                K.op("pe", lambda e, k=k, mm=mm: e.matmul(pg[:], lhsT=wg[:, k, mm * 128:(mm + 1) * 128], rhs=h2T[:, k, :], start=(k == 0), stop=(k == 15)),
                                     reads=[wgr, h2Tr], writes=[pgr] if k == 0 else (), more=() if k == 0 else [pgr], inc=(k == 15))
                            pu, pur = ps_next()
                            for k in range(16):
                                K.op("pe", lambda e, k=k, mm=mm: e.matmul(pu[:], lhsT=wu[:, k, mm * 128:(mm + 1) * 128], rhs=h2T[:, k, :], start=(k == 0), stop=(k == 15)),
                                     reads=[wur, h2Tr], writes=[pur] if k == 0 else (), more=() if k == 0 else [pur], inc=(k == 15))
                            sg, sgr = sgt[m % 2], sgtr[m % 2]
                            K.op("act", lambda e: e.activation(out=sg[:], in_=pg[:], func=AF.Silu), reads=[pgr], writes=[sgr])
                            K.op("dve", lambda e, m=m: e.tensor_tensor(out=aT[:, m, :], in0=sg[:], in1=pu[:], op=ALU.mult), reads=[sgr, pur],
                                 writes=[aTr] if m == 0 else (), more=() if m == 0 else [aTr])
                        w_done(); w_done()
                    for n in range(4):
                        pbs = [ps_next() for _ in range(4)]
                        for (k0, nk) in ((0, 16), (16, 16), (32, 12)):
                            wd, wdr = w_get((w_down, k0, nk, n * 512, 512))
                            for b in range(4):
                                pb, pr = pbs[b]
                                for kk in range(nk):
                                    k = k0 + kk
                                    K.op("pe", lambda e, k=k, kk=kk, b=b, pb=pb: e.matmul(pb[:], lhsT=aT[:, k, b * 128:(b + 1) * 128], rhs=wd[:, kk, :], start=(k == 0), stop=(k == 43)),
                                         reads=[wdr, aTr], writes=[pr] if k == 0 else (), more=() if k == 0 else [pr], inc=(kk == nk - 1))
                            w_done()
                        for b in range(4):
                            pb, pr = pbs[b]
                            xa = xt[:, b, n * 512:(n + 1) * 512]
                            K.op("dve", lambda e, xa=xa, pb=pb: e.tensor_tensor(out=xa, in0=pb[:], in1=xa, op=ALU.add), reads=[pr], writes=[xtr[b]])
                    for b in range(4):
                        bi = t * 4 + b
                        dres = R["xres"][s] if not last else R.setdefault("out", K.res("out", local=False))
                        K.dma("sp", dst_x[bi * 128:(bi + 1) * 128, :], xt[:, b, :], reads=[xtr[b]], more=[dres], semres=dres)
                K.barrier()
            ring["n"] = 6

        stop = DBG["stop_after"]
        done = False
        for l in range(L):
            for s in range(NS):
                for nm, fn in (("p1", lambda: phase1(l, s)), ("p2B", lambda: phase2(l, s, "B")),
                               ("p2D", lambda: phase2(l, s, "D")), ("p3", lambda: phase3(l, s))):
                    if nm == "p3" and stop in ("p1", "p1b", "p2B", "p2D"):
                        continue
                    fn()
                    if stop == nm:
                        done = True
                        break
                if done:
                    break
            if done:
                break
        for key in list(K.semh.keys()):
            K.wait("sp", key, K.semtot[key])
        for f in ("pe", "act", "dve", "pool"):
            if K.E[f].count:
                K.wait("sp", f, K.E[f].count)
    return nc


def _t5_bucket_np(rel):
    nb = 16
    max_exact = 8
    ret = np.where(rel > 0, nb, 0)
    n = np.abs(rel)
    nf = np.maximum(n, 1).astype(np.float32)
    large = max_exact + (np.log(nf / np.float32(max_exact)) / np.float32(math.log(1024 / max_exact)) * np.float32(nb - max_exact)).astype(np.int32)
    large = np.minimum(large, nb - 1)
    return ret + np.where(n < max_exact, n, large)


def _host_consts(rel_bias):
    ident = np.eye(128, dtype=np.float32)
    anti = np.ascontiguousarray(ident[::-1])
    pos = np.arange(S)
    row = (pos // 64).astype(np.float32)
    col = (pos % 64).astype(np.float32)
    freqs = (np.float32(10000.0) ** (-np.arange(16, dtype=np.float32) / np.float32(16))).astype(np.float32)
    ar = row[:, None] * freqs[None]
    ac = col[:, None] * freqs[None]
    cos = np.concatenate([np.cos(ar), np.cos(ar), np.cos(ac), np.cos(ac)], axis=1).astype(np.float32)
    sin = np.concatenate([-np.sin(ar), np.sin(ar), -np.sin(ac), np.sin(ac)], axis=1).astype(np.float32)
    u = np.arange(4096)
    rel = 2047 - u
    mult = ((np.abs(rel) <= 64).astype(np.int32) + ((rel % 4 == 0) & (np.abs(rel) <= 256)).astype(np.int32)
            + ((rel % 16 == 0) & (np.abs(rel) <= 1024)).astype(np.int32))
    lm = np.where(mult > 0, np.log(np.maximum(mult, 1).astype(np.float32)), np.float32(-1e30)).astype(np.float32)
    lmult = np.ascontiguousarray(np.broadcast_to(lm[None], (8, 4096))).astype(np.float32)
    bk = _t5_bucket_np(np.clip(rel, -2047, 2047))
    rbg = np.ascontiguousarray(np.asarray(rel_bias, dtype=np.float32)[bk, :].T)
    return {"c_ident": ident, "c_anti": anti, "c_cos": cos, "c_sin": sin, "c_rbg": rbg, "c_lmult": lmult}


def kernel(x, rel_bias, norm1_g, w_in, sgu_w, sgu_b, dil_qn_g, dil_kn_g, conv_w, conv_b, conv_ln_g, conv_ln_b,
           gqa_qn_g, gqa_kn_g, mix_norm_g, w_out, norm2_g, w_gate, w_up, w_down):
    ncores = DBG["ncores"]
    x = np.ascontiguousarray(np.asarray(x, dtype=np.float32))
    shared = {"norm1_g": norm1_g, "w_in": w_in, "sgu_w": sgu_w, "sgu_b": sgu_b, "dil_qn_g": dil_qn_g, "dil_kn_g": dil_kn_g,
              "conv_w": conv_w, "conv_b": conv_b, "conv_ln_g": conv_ln_g, "conv_ln_b": conv_ln_b, "gqa_qn_g": gqa_qn_g,
              "gqa_kn_g": gqa_kn_g, "mix_norm_g": mix_norm_g, "w_out": w_out, "norm2_g": norm2_g, "w_gate": w_gate,
              "w_up": w_up, "w_down": w_down}
    shared = {k: np.ascontiguousarray(np.asarray(v, dtype=np.float32)) for k, v in shared.items()}
    shared.update(_host_consts(rel_bias))
    nc = build_program()
    in_maps = []
    for c in range(ncores):
        m = dict(shared)
        m["x"] = x[c * NSEQ:(c + 1) * NSEQ]
        in_maps.append(m)
    res = run_bass_kernel_spmd(nc, in_maps, core_ids=list(range(ncores)))
    if DBG["dump"]:
        return res
    return np.concatenate([r["out"] for r in res.results], axis=0)
```
